# Optimizing a Trainium2 kernel written in Bass

```python
import math
import jax, jax.numpy as jnp
from jax import lax
import numpy as np

D_MODEL = 1024
BATCH = 8
SEQ = 2048
DEPTH = 1

CHUNK = 64
D_MIX = 2 * D_MODEL
D_MLSTM = D_MIX // 2
D_CONV = D_MIX - D_MLSTM
N_HEADS = 4
HEAD_DIM = D_MLSTM // N_HEADS
CONV_WIDTH = 31
D_IN_PROJ = 5 * D_MLSTM + 2 * N_HEADS + 3 * D_CONV
EPS = 1e-6
M_INIT = -1e30

kernel_name = "hybrid_mlstm_conformer_conv_block"


def rmsnorm(x, g):
    xf = x.astype(jnp.float32)
    y = xf * lax.rsqrt(jnp.mean(xf * xf, axis=-1, keepdims=True) + EPS)
    return (y * g.astype(jnp.float32)).astype(x.dtype)


def layernorm(x, g, b):
    xf = x.astype(jnp.float32)
    mu = jnp.mean(xf, axis=-1, keepdims=True)
    xc = xf - mu
    var = jnp.mean(xc * xc, axis=-1, keepdims=True)
    y = xc * lax.rsqrt(var + EPS) * g.astype(jnp.float32) + b.astype(jnp.float32)
    return y.astype(x.dtype)


def mlstm_chunkwise(q, k, v, i_pre, f_pre):
    B, S, _ = q.shape
    NC = S // CHUNK
    L = CHUNK

    def heads(t):
        return t.astype(jnp.float32).reshape(B, NC, L, N_HEADS, HEAD_DIM).transpose(0, 3, 1, 2, 4)

    def gate_heads(t):
        return t.astype(jnp.float32).reshape(B, NC, L, N_HEADS).transpose(0, 3, 1, 2)

    qh = heads(q)
    kh = heads(k) * (HEAD_DIM ** -0.5)
    vh = heads(v)
    ig = gate_heads(i_pre)
    lf = jax.nn.log_sigmoid(gate_heads(f_pre))

    b = jnp.cumsum(lf, axis=-1)
    b_last = b[..., -1]

    a = b_last[..., None] - b + ig
    a_max = jnp.max(a, axis=-1)
    w_a = jnp.exp(a - a_max[..., None])
    kv_chunk = jnp.einsum('bhcl,bhcld,bhcle->bhcde', w_a, kh, vh)
    n_chunk = jnp.einsum('bhcl,bhcld->bhcd', w_a, kh)

    def step(carry, xs):
        C, n, m = carry
        kv_c, n_c, bl, am = xs
        m_new = jnp.maximum(bl + m, am)
        s_old = jnp.exp(bl + m - m_new)
        s_new = jnp.exp(am - m_new)
        C_new = s_old[..., None, None] * C + s_new[..., None, None] * kv_c
        n_new = s_old[..., None] * n + s_new[..., None] * n_c
        return (C_new, n_new, m_new), (C, n, m)

    init = (jnp.zeros((B, N_HEADS, HEAD_DIM, HEAD_DIM), jnp.float32),
            jnp.zeros((B, N_HEADS, HEAD_DIM), jnp.float32),
            jnp.full((B, N_HEADS), M_INIT, jnp.float32))
    xs = (jnp.moveaxis(kv_chunk, 2, 0), jnp.moveaxis(n_chunk, 2, 0),
          jnp.moveaxis(b_last, 2, 0), jnp.moveaxis(a_max, 2, 0))
    _, (C_prev, n_prev, m_prev) = lax.scan(step, init, xs)
    C_prev = jnp.moveaxis(C_prev, 0, 2)
    n_prev = jnp.moveaxis(n_prev, 0, 2)
    m_prev = jnp.moveaxis(m_prev, 0, 2)

    causal = jnp.tril(jnp.ones((L, L), dtype=bool))
    g = b[..., :, None] - b[..., None, :] + ig[..., None, :]
    g = jnp.where(causal, g, -jnp.inf)
    li = b + m_prev[..., None]
    m_t = jnp.maximum(li, jnp.max(g, axis=-1))

    scores = jnp.einsum('bhcld,bhcsd->bhcls', qh, kh)
    w = jnp.exp(g - m_t[..., None]) * scores
    s_inter = jnp.exp(li - m_t)
    num = (jnp.einsum('bhcls,bhcse->bhcle', w, vh)
           + s_inter[..., None] * jnp.einsum('bhcld,bhcde->bhcle', qh, C_prev))
    den = jnp.sum(w, axis=-1) + s_inter * jnp.einsum('bhcld,bhcd->bhcl', qh, n_prev)
    h = num / jnp.maximum(jnp.abs(den), jnp.exp(-m_t))[..., None]
    return h.transpose(0, 2, 3, 1, 4).reshape(B, S, N_HEADS * HEAD_DIM)


def setup_inputs(seed: int = 0) -> dict:
    key = jax.random.key(seed)
    ks = jax.random.split(key, 12)
    f32 = jnp.float32
    x = jax.random.normal(ks[0], (BATCH, SEQ, D_MODEL), f32)
    norm_g = 1.0 + 0.02 * jax.random.normal(ks[1], (DEPTH, D_MODEL), f32)
    w_in = jax.random.normal(ks[2], (DEPTH, D_MODEL, D_IN_PROJ), f32) * D_MODEL ** -0.5
    i_bias = 0.1 * jax.random.normal(ks[3], (DEPTH, N_HEADS), f32)
    f_bias = (jnp.linspace(3.0, 6.0, N_HEADS, dtype=f32)[None, :]
              + 0.1 * jax.random.normal(ks[4], (DEPTH, N_HEADS), f32))
    b_gates = jnp.concatenate([i_bias, f_bias], axis=-1)
    mh_norm_g = 1.0 + 0.02 * jax.random.normal(ks[5], (DEPTH, D_MLSTM), f32)
    conv_w = jax.random.normal(ks[6], (DEPTH, CONV_WIDTH, D_CONV), f32) * CONV_WIDTH ** -0.5
    conv_b = 0.02 * jax.random.normal(ks[7], (DEPTH, D_CONV), f32)
    conv_ln_g = 1.0 + 0.02 * jax.random.normal(ks[8], (DEPTH, D_CONV), f32)
    conv_ln_b = 0.02 * jax.random.normal(ks[9], (DEPTH, D_CONV), f32)
    w_out = jax.random.normal(ks[10], (DEPTH, D_MIX, D_MODEL), f32) * D_MIX ** -0.5
    final_norm_g = 1.0 + 0.02 * jax.random.normal(ks[11], (D_MODEL,), f32)
    return {"x": x, "norm_g": norm_g, "w_in": w_in, "b_gates": b_gates,
            "mh_norm_g": mh_norm_g, "conv_w": conv_w, "conv_b": conv_b,
            "conv_ln_g": conv_ln_g, "conv_ln_b": conv_ln_b, "w_out": w_out,
            "final_norm_g": final_norm_g}


def reference(x, norm_g, w_in, b_gates, mh_norm_g, conv_w, conv_b, conv_ln_g,
              conv_ln_b, w_out, final_norm_g):
    B, S, _ = x.shape
    sizes = [D_MLSTM] * 5 + [N_HEADS, N_HEADS] + [D_CONV] * 3
    split_idx = [int(s) for s in np.cumsum(sizes)[:-1]]
    h = x
    for l in range(DEPTH):
        u = rmsnorm(h, norm_g[l])
        proj = jnp.einsum('bsd,de->bse', u, w_in[l])
        (q, k, v, o_pre, z_m, i_pre, f_pre,
         glu_a, glu_g, z_c) = jnp.split(proj, split_idx, axis=-1)

        i_pre = i_pre + b_gates[l, :N_HEADS]
        f_pre = f_pre + b_gates[l, N_HEADS:]
        hm = mlstm_chunkwise(q, k, v, i_pre, f_pre)
        hm = jax.nn.sigmoid(o_pre.astype(jnp.float32)) * hm
        hm = hm.reshape(B, S, N_HEADS, HEAD_DIM)
        hm = hm * lax.rsqrt(jnp.mean(hm * hm, axis=-1, keepdims=True) + EPS)
        hm = hm.reshape(B, S, D_MLSTM) * mh_norm_g[l].astype(jnp.float32)
        hm = (hm * jax.nn.silu(z_m.astype(jnp.float32))).astype(x.dtype)

        c = glu_a * jax.nn.sigmoid(glu_g)
        c = lax.conv_general_dilated(
            c, conv_w[l][:, None, :], window_strides=(1,),
            padding=[(CONV_WIDTH - 1, 0)],
            dimension_numbers=('NWC', 'WIO', 'NWC'),
            feature_group_count=D_CONV) + conv_b[l]
        c = layernorm(c, conv_ln_g[l], conv_ln_b[l])
        c = jax.nn.silu(c) * jax.nn.silu(z_c)

        mix = jnp.concatenate([hm, c.astype(x.dtype)], axis=-1)
        h = h + jnp.einsum('bse,ed->bsd', mix, w_out[l])
    return rmsnorm(h, final_norm_g)
```

```python
import os
import numpy as np
import concourse.bass as bass
import concourse.mybir as mybir
from concourse.bass_utils import run_bass_kernel_spmd

F32 = mybir.dt.float32
BF16 = mybir.dt.bfloat16
U8 = mybir.dt.uint8
ALU = mybir.AluOpType
AF = mybir.ActivationFunctionType
AX = mybir.AxisListType

S = 2048
D = 1024
NT = 16
NB = 4
DIN = 8200
EPS = 1e-6
C_Q, C_K, C_V, C_O, C_ZM, C_I, C_GA, C_GG, C_ZC = 0, 1024, 2048, 3072, 4096, 5120, 5128, 6152, 7176


class Buf:
    __slots__ = ("lw", "rd", "excl")

    def __init__(self, excl=False):
        self.lw = None
        self.rd = {}
        self.excl = excl


class Sched:
    def __init__(self, nc, block):
        self.nc = nc
        self.fn = {"pe": block.tensor, "act": block.scalar, "dve": block.vector,
                   "pool": block.gpsimd, "sp": block.sync}
        self.semh = {}
        self.cnt = {}
        for e in ("pe", "act", "dve", "pool"):
            self.semh[e] = nc.alloc_semaphore("s_" + e)
            self.cnt[e] = 0
        self.waited = {e: {} for e in self.fn}
        self.dq = {}
        for q, n in (("sp", 8), ("pool", 8), ("act", 4)):
            lst = []
            for i in range(n):
                key = "d_%s%d" % (q, i)
                self.semh[key] = nc.alloc_semaphore(key)
                self.cnt[key] = 0
                lst.append(key)
            self.dq[q] = [lst, 0]
        self.bufs = {}
        self.q = {e: [] for e in self.fn}

    def flush(self):
        for eng, lst in self.q.items():
            if lst:
                def run_all(e, lst=lst):
                    for f in lst:
                        f(e)
                self.fn[eng](run_all)
            self.q[eng] = []

    def B(self, *name):
        b = self.bufs.get(name)
        if b is None:
            b = self.bufs[name] = Buf(excl=(name[0] == "ps"))
        return b

    def _deps(self, eng, reads, writes, extra):
        deps = {}

        def add(tok):
            if tok is None:
                return
            k, v = tok
            if v > deps.get(k, 0):
                deps[k] = v
        xr = [b for b in reads if b.excl]
        for b in reads:
            add(b.lw)
        for b in list(writes) + xr:
            add(b.lw)
            for k, v in b.rd.items():
                add((k, v))
        for t in extra:
            add(t)
        waits = []
        w = self.waited[eng]
        for k, v in deps.items():
            if eng == "pe" and k == "pe":
                continue
            if w.get(k, 0) >= v:
                continue
            w[k] = v
            waits.append((self.semh[k], v))
        return waits

    def _mark(self, tok, reads, writes):
        k, v = tok
        for b in reads:
            if b.excl:
                continue
            if v > b.rd.get(k, 0):
                b.rd[k] = v
        for b in list(writes) + [b for b in reads if b.excl]:
            b.lw = tok
            b.rd = {}

    def op(self, eng, fn, reads=(), writes=(), extra=()):
        waits = self._deps(eng, reads, writes, extra)
        self.cnt[eng] += 1
        tok = (eng, self.cnt[eng])
        sem = self.semh[eng]

        def run(e):
            for s, v in waits:
                e.wait_ge(s, v)
            fn(e).then_inc(sem, 1)
        self.q[eng].append(run)
        self._mark(tok, reads, writes)
        return tok

    def dma(self, q, fn, reads=(), writes=(), extra=()):
        lst, idx = self.dq[q]
        key = lst[idx % len(lst)]
        self.dq[q][1] = idx + 1
        prior = self.cnt[key]
        ex = list(extra)
        if prior:
            ex.append((key, prior))
        waits = self._deps(q, reads, writes, ex)
        self.cnt[key] = prior + 16
        tok = (key, prior + 16)
        sem = self.semh[key]

        def run(e):
            for s, v in waits:
                e.wait_ge(s, v)
            fn(e).then_inc(sem, 16)
        self.q[q].append(run)
        self._mark(tok, reads, writes)
        return tok

    def barrier(self):
        cur = [(k, v) for k, v in self.cnt.items() if v > 0]
        for eng in self.fn:
            waits = []
            w = self.waited[eng]
            for k, v in cur:
                if w.get(k, 0) >= v:
                    continue
                w[k] = v
                waits.append((self.semh[k], v))
            if waits:
                def run(e, waits=waits):
                    for s, v in waits:
                        e.wait_ge(s, v)
                self.q[eng].append(run)

    def wait_all(self, eng, toks):
        waits = []
        w = self.waited[eng]
        for k, v in toks:
            if w.get(k, 0) >= v:
                continue
            w[k] = v
            waits.append((self.semh[k], v))

        def run(e):
            for s, v in waits:
                e.wait_ge(s, v)
        if waits:
            self.q[eng].append(run)


def mm_group(e, out, pairs):
    n = len(pairs)
    ins = None
    for i, (l, r) in enumerate(pairs):
        ins = e.matmul(out, l, r, start=(i == 0), stop=(i == n - 1))
    return ins


def build(debug=()):
    nc = bass.Bass("TRN2", target_bir_lowering=False)
    x = nc.dram_tensor("x", [S, D], F32, kind="ExternalInput").ap()
    norm_g = nc.dram_tensor("norm_g", [1, D], F32, kind="ExternalInput").ap()
    w_in = nc.dram_tensor("w_in", [D, DIN], F32, kind="ExternalInput").ap()
    b_gates = nc.dram_tensor("b_gates", [1, 8], F32, kind="ExternalInput").ap()
    mh_g = nc.dram_tensor("mh_norm_g", [1, D], F32, kind="ExternalInput").ap()
    conv_w = nc.dram_tensor("conv_w", [31, D], F32, kind="ExternalInput").ap()
    conv_b = nc.dram_tensor("conv_b", [1, D], F32, kind="ExternalInput").ap()
    ln_g = nc.dram_tensor("conv_ln_g", [1, D], F32, kind="ExternalInput").ap()
    ln_b = nc.dram_tensor("conv_ln_b", [1, D], F32, kind="ExternalInput").ap()
    w_out = nc.dram_tensor("w_out", [2 * D, D], F32, kind="ExternalInput").ap()
    fin_g = nc.dram_tensor("final_norm_g", [1, D], F32, kind="ExternalInput").ap()
    out = nc.dram_tensor("out", [S, D], F32, kind="ExternalOutput").ap()
    dbg_out = {}
    carve_log = []

    ARENA = 12288 + 32768 + 65536 + 28672 + 32768 + 39552
    arena = nc.alloc_sbuf_tensor("arena", [128, ARENA], U8).ap()

    def carve(off, nbytes, dt, shape=None):
        ap = arena[:, off:off + nbytes].bitcast(dt)
        carve_log.append((off, nbytes))
        if shape is not None and len(shape) == 2:
            ap = ap.rearrange("p (a b) -> p a b", a=shape[0])
        elif shape is not None and len(shape) == 3:
            ap = ap.rearrange("p (a b c) -> p a b c", a=shape[0], b=shape[1])
        return ap

    o = 0
    ident_bf = carve(o, 256, BF16); o += 256
    ident_f = carve(o, 512, F32); o += 512
    maskT = carve(o, 512, F32); o += 512
    ones_f = carve(o, 512, F32); o += 512
    onesb = carve(o, 256, BF16); o += 256
    convbT = carve(o, 32, F32); o += 32
    lngT = carve(o, 32, F32); o += 32
    lnbT = carve(o, 32, F32); o += 32
    mhgT = carve(o, 32, F32); o += 32
    nhalf = carve(o, 32, F32); o += 32
    Rrep = carve(o, 64, BF16); o += 64
    o += 32
    wconvT = carve(o, 1024, F32, (8, 32)); o += 1024
    gbc = carve(o, 4096, F32); o += 4096
    Gt = carve(o, 512, F32); o += 512
    bgbc = carve(o, 512, F32); o += 512
    gt = {}
    for nm in ("af", "ex", "l1", "lf", "carry", "Bt", "a", "Mbc", "e", "e2", "r", "fl", "d"):
        gt[nm] = carve(o, 256, F32); o += 256
    cmax = carve(o, 32, F32); o += 32
    minc = carve(o, 256, F32); o += 256
    o = (o + 255) // 256 * 256
    assert o <= 12288, o
    R_UT = 12288
    R_MIX = R_UT + 32768
    R_W = R_MIX + 65536
    R_Y = R_W + 28672
    R_C = R_Y + 32768
    assert R_C + 39552 <= ARENA

    uT = carve(R_UT, 32768, BF16, (8, 2048))
    mixT = carve(R_MIX, 65536, BF16, (16, 2048))
    y_lo = carve(R_MIX, 32768, F32, (4, 2048))
    y_hi = carve(R_Y, 32768, F32, (4, 2048))
    wslot = [carve(R_W + i * 4096, 4096, BF16, (8, 256)) for i in range(7)]
    woutT = carve(R_Y, 32768, BF16, (16, 1024))

    def ysl(cc, tb):
        t = y_lo if cc < 4 else y_hi
        return t[:, cc % 4, tb * 512:(tb + 1) * 512]

    xs = [carve(R_C + i * 4096, 4096, F32) for i in range(3)]
    xs0 = [carve(R_Y + i * 4096, 4096, F32) for i in range(5)]
    xn = [carve(R_Y + 20480 + i * 2048, 2048, BF16) for i in range(3)]
    junk = carve(R_Y + 26624, 2048, BF16)
    ssq = carve(R_Y + 28672, 64, F32)
    srt = carve(R_Y + 28736, 64, F32)
    rstd0 = carve(R_Y + 28800, 64, F32)
    cw = carve(R_C + 33152, 4096, F32)
    hres = [carve(R_C + 12288 + i * 4096, 4096, F32) for i in range(2)]
    outt = [carve(R_C + 20480 + i * 4096, 4096, F32) for i in range(2)]
    junk4 = carve(R_C + 28672, 2048, BF16)
    ssq4 = carve(R_C + 30720, 64, F32)
    srt4 = carve(R_C + 30784, 64, F32)
    rstd4 = carve(R_C + 30848, 64, F32)
    Xt = [carve(R_C + i * 4160, 4160, BF16) for i in range(4)]
    Wgp = [carve(R_C + 29056 + i * 2048, 2048, BF16, (32, 32)) for i in range(2)]
    wcolT = carve(R_C + 38528, 1024, F32)
    Pq = [carve(R_MIX + i * 896, 896, F32) for i in range(4)]
    cbuf = [carve(R_C + 16640 + i * 4160, 4160, BF16) for i in range(2)]
    thb = [carve(R_C + 24960 + i * 2048, 2048, F32) for i in range(2)]
    ybb = [carve(R_C + 33152 + i * 1024, 1024, BF16) for i in range(3)]
    ysqb = [carve(R_C + 36224 + i * 1024, 1024, BF16) for i in range(2)]
    mu = carve(R_C, 8192, F32)
    var = carve(R_C + 8192, 8192, F32)
    tnb = [carve(R_C + 16384 + i * 2048, 2048, F32) for i in range(4)]
    szb = [carve(R_C + 24576 + i * 2048, 2048, F32) for i in range(3)]
    qT = carve(R_C, 8192, BF16, (2, 2048))
    kT = carve(R_C + 8192, 8192, BF16, (2, 2048))
    vext = carve(R_C + 16384, 8256, BF16, (16, 258))
    Cm = carve(R_C + 24640, 2080, F32, (2, 260))
    Cbf = [carve(R_C + 26720 + i * 1040, 1040, BF16, (2, 260)) for i in range(2)]
    tho = [carve(R_C + 28800 + i * 1024, 1024, F32) for i in range(2)]
    szm = [carve(R_C + 30848 + i * 1024, 1024, F32) for i in range(2)] + [carve(R_C + 38528, 1024, F32)]
    Ab = [carve(R_C + 32896 + i * 1024, 1024, F32) for i in range(2)]
    wTb = [carve(R_C + 34944 + i * 256, 256, BF16) for i in range(2)]
    ekb = [carve(R_C + 35456 + i * 512, 512, BF16) for i in range(2)]
    hmb = [carve(R_C + 36480 + i * 512, 512, BF16) for i in range(2)]
    tiny = [carve(R_C + 37504 + i * 64, 64, F32) for i in range(8)]
    junkm = carve(R_C + 38016, 512, BF16)
    assert R_C + 39552 <= ARENA

    PB = [nc.alloc_psum_tensor("pb%d" % i, [128, 512], F32) for i in range(8)]
    PBf = [p.ap() for p in PB]
    PBb = [p.bitcast(BF16).ap() for p in PB]

    out_toks = []
    with nc.Block() as block:
        K = Sched(nc, block)
        B = K.B

        def c_ones(e):
            e.memset(ones_f, 1.0)
            e.memset(onesb, 1.0 / 1024.0)
            e.memset(nhalf, -0.5)
            return e.memset(gt["carry"][:, 0:4], 0.0)
        K.op("pool", c_ones, writes=[B("ones")])
        K.op("pool", lambda e: e.affine_select(out=ident_f, in_=ones_f, pattern=[[-1, 128]],
                                               compare_op=ALU.is_equal, fill=0.0, base=0,
                                               channel_multiplier=1),
             reads=[B("ones")], writes=[B("ident_f")])
        K.op("pool", lambda e: e.affine_select(out=maskT, in_=ones_f, pattern=[[1, 128]],
                                               compare_op=ALU.is_ge, fill=0.0, base=0,
                                               channel_multiplier=-1),
             reads=[B("ones")], writes=[B("maskT")])
        K.op("pool", lambda e: e.tensor_copy(out=ident_bf, in_=ident_f),
             reads=[B("ident_f")], writes=[B("ident_bf")])

        def bc_row(ap_dram, n):
            return bass.AP(ap_dram.tensor, 0, [[0, 128], [1, n]])

        def col_vec(ap_dram):
            return bass.AP(ap_dram.tensor, 0, [[1, 128], [128, 8]])
        K.dma("act", lambda e: e.dma_start(out=gbc, in_=bc_row(norm_g, D)), writes=[B("gbc")])
        K.dma("act", lambda e: e.dma_start(out=cw[0:31, :], in_=conv_w), writes=[B("cw")])

        w_in_v = w_in.rearrange("(dc p) n -> p dc n", p=128)

        def load_w(slot, parts, name):
            tok = None
            for (dc0, sc0, ncol) in parts:
                tok = K.dma("pool", lambda e, dc0=dc0, sc0=sc0, ncol=ncol: e.dma_start(
                    out=wslot[slot][:, :, dc0:dc0 + ncol], in_=w_in_v[:, :, sc0:sc0 + ncol]),
                    writes=[B("w", slot)])
            return tok

        def p0_front(tt):
            s5, s3 = tt % 5, tt % 3
            K.dma("sp", lambda e: e.dma_start(out=xs0[s5], in_=x[tt * 128:(tt + 1) * 128, :]),
                  writes=[B("xs0", s5)])
            K.op("act", lambda e: e.activation(out=junk, in_=xs0[s5], func=AF.Square,
                                               accum_out=ssq[:, tt:tt + 1]),
                 reads=[B("xs0", s5)], writes=[B("junk"), B("ssq", tt)])
            K.op("act", lambda e: e.activation(out=srt[:, tt:tt + 1], in_=ssq[:, tt:tt + 1],
                                               func=AF.Sqrt, scale=1.0 / D, bias=EPS),
                 reads=[B("ssq", tt)], writes=[B("srt", tt)])
            K.op("dve", lambda e: e.reciprocal(out=rstd0[:, tt:tt + 1], in_=srt[:, tt:tt + 1]),
                 reads=[B("srt", tt)], writes=[B("rstd0", tt)])
            K.op("dve", lambda e: e.scalar_tensor_tensor(
                out=xn[s3], in0=xs0[s5], scalar=rstd0[:, tt:tt + 1], in1=gbc, op0=ALU.mult, op1=ALU.mult),
                reads=[B("xs0", s5), B("rstd0", tt), B("gbc")], writes=[B("xn", s3)])
            pt = PBb[6 + tt % 2]

            def tr(e):
                ins = None
                for dc in range(8):
                    ins = e.transpose(out=pt[:, dc * 128:(dc + 1) * 128],
                                      in_=xn[s3][:, dc * 128:(dc + 1) * 128], identity=ident_bf)
                return ins
            K.op("pe", tr, reads=[B("xn", s3), B("ident_bf")], writes=[B("ps", 6 + tt % 2)])

        def p0_back(tt):
            pt = PBb[6 + tt % 2]
            if False:
                f = lambda e: e.activation(
                    out=uT[:, :, tt * 128:(tt + 1) * 128], in_=pt.rearrange("p (a b) -> p a b", a=8), func=AF.Copy)
                K.op("act", f, reads=[B("ps", 6 + tt % 2)], writes=[B("uT", tt)])
            else:
                f = lambda e: e.tensor_copy(
                    out=uT[:, :, tt * 128:(tt + 1) * 128], in_=pt.rearrange("p (a b) -> p a b", a=8))
                K.op("dve", f, reads=[B("ps", 6 + tt % 2)], writes=[B("uT", tt)])
        load_w(0, [(0, C_GA, 128), (128, C_GG, 128)], "glu")

        def tile_proj(tt):
            sb = (tt // 4) % 2
            c0 = (tt % 4) * 128
            ws0 = wslot[0]

            def f(e):
                mm_group(e, PBf[2 + sb][:, c0:c0 + 128],
                         [(ws0[:, dc, 128:256], uT[:, dc, tt * 128:(tt + 1) * 128]) for dc in range(8)])
                return mm_group(e, PBf[sb][:, c0:c0 + 128],
                                [(ws0[:, dc, 0:128], uT[:, dc, tt * 128:(tt + 1) * 128]) for dc in range(8)])
            K.op("pe", f, reads=[B("uT", tt), B("w", 0)], writes=[B("ps", sb), B("ps", 2 + sb)])

        def tile_ev(tb):
            sb = tb % 2
            K.op("act", lambda e: e.activation(out=thb[sb], in_=PBf[2 + sb], func=AF.Tanh, scale=0.5),
                 reads=[B("ps", 2 + sb)], writes=[B("th", sb)])
            K.op("dve", lambda e: e.scalar_tensor_tensor(
                out=cbuf[0][:, 30 + tb * 512:30 + (tb + 1) * 512], in0=thb[sb], scalar=1.0, in1=PBf[sb],
                op0=ALU.add, op1=ALU.mult),
                reads=[B("th", sb), B("ps", sb)], writes=[B("cbuf", 0, tb)])
        pend_ev = []
        for i in range(NT + 5):
            if i < NT:
                p0_front(i)
            if 1 <= i <= NT:
                p0_back(i - 1)
            if 0 <= i - 2 < NT:
                tile_proj(i - 2)
                if (i - 2) % 4 == 3:
                    pend_ev.append((i + 2, (i - 2) // 4))
            while pend_ev and pend_ev[0][0] <= i:
                tile_ev(pend_ev.pop(0)[1])
        while pend_ev:
            tile_ev(pend_ev.pop(0)[1])

        for nm, dst, src in (("convbT", convbT, conv_b), ("lngT", lngT, ln_g),
                             ("lnbT", lnbT, ln_b), ("mhgT", mhgT, mh_g)):
            K.dma("sp", lambda e, dst=dst, src=src: e.dma_start(
                out=dst, in_=col_vec(src), allow_slow_non_contiguous=True), writes=[B(nm)])
        K.dma("sp", lambda e: e.dma_start(
            out=bgbc.rearrange("p (t j) -> p t j", j=8),
            in_=bass.AP(b_gates.tensor, 0, [[0, 128], [0, 16], [1, 8]])), writes=[B("bgbc")])


        def trw(e):
            ins = None
            for cc in range(8):
                ins = e.transpose(out=PBf[5][:, cc * 32:cc * 32 + 31], in_=cw[0:31, cc * 128:(cc + 1) * 128],
                                  identity=ident_f[0:31, 0:31])
            return ins
        K.op("pe", trw, reads=[B("cw"), B("ident_f")], writes=[B("ps", 5)])
        K.op("pool", lambda e: e.memset(wconvT, 0.0), writes=[B("wconvT")])
        K.op("dve", lambda e: e.tensor_scalar(
            out=wconvT[:, :, 0:31], in0=PBf[5][:, 0:256].rearrange("p (a b) -> p a b", a=8)[:, :, 0:31],
            scalar1=0.5, scalar2=None, op0=ALU.mult), reads=[B("ps", 5)], writes=[B("wconvT")])

        def mk_sel0(e):
            ins = None
            for q in range(4):
                ins = e.memset(Pq[q], 0.0)
            return ins
        K.op("pool", mk_sel0, writes=[B("Pq")])

        def mk_sel(e):
            ins = None
            for q in range(4):
                ins = e.tensor_copy(out=Pq[q][:, 96:128], in_=ident_f[:, 32 * q:32 * q + 32])
            return ins
        K.op("pool", mk_sel, reads=[B("ident_f")], writes=[B("Pq")])
        K.op("pool", lambda e: e.tensor_copy(out=Rrep, in_=ident_bf[:, 0:32]), reads=[B("ident_bf")], writes=[B("Rrep")])
        for j in range(1, 4):
            K.op("pool", lambda e, j=j: e.tensor_tensor(out=Rrep, in0=Rrep, in1=ident_bf[:, 32 * j:32 * j + 32], op=ALU.add),
                 reads=[B("Rrep"), B("ident_bf")], writes=[B("Rrep")])

        def wcol_mm(e):
            ins = None
            for q in range(4):
                for jlo in range(4):
                    ins = e.matmul(PBf[5][:, q * 64:(q + 1) * 64], Pq[q][:, 96 - 32 * jlo:224 - 32 * jlo],
                                   wconvT[:, :, jlo:32:4], start=(jlo == 0), stop=(jlo == 3))
            return ins
        K.op("pe", wcol_mm, reads=[B("wconvT"), B("Pq")], writes=[B("ps", 5)])
        K.op("dve", lambda e: e.tensor_copy(
            out=wcolT.rearrange("p (c q g) -> p q c g", c=8, q=4),
            in_=PBf[5][:, 0:256].rearrange("p (q c g) -> p q c g", q=4, c=8)),
            reads=[B("ps", 5)], writes=[B("wcolT")])

        uT_all = [B("uT", t) for t in range(NT)]

        def uT_blk(tb):
            return [B("uT", t) for t in range(tb * 4, tb * 4 + 4)]

        Wg = carve(R_W + 5 * 4096, 4096, BF16, (8, 256))
        steps = []

        def st(fn):
            steps.append(fn)

        def st_pe(fn):
            steps.extend([None, None, None])
            steps.append(fn)
        G3 = Gt.rearrange("p (t j) -> p t j", j=8)
        ipre = G3[:, :, 0:4]
        fpre = G3[:, :, 4:8]

        def v3(ap):
            return ap.rearrange("p (t j) -> p t j", j=4)
        K.dma("pool", lambda e: e.dma_start(out=Wg[:, :, 0:8], in_=w_in_v[:, :, C_I:C_I + 8]),
              writes=[B("w", 5)])

        def gate_mm(e):
            ins = None
            for tt in range(NT):
                ins = mm_group(e, PBf[7][:, tt * 8:(tt + 1) * 8],
                               [(uT[:, dc, tt * 128:(tt + 1) * 128], Wg[:, dc, 0:8]) for dc in range(8)])
            return ins
        st_pe(lambda: K.op("pe", gate_mm, reads=uT_all + [B("w", 5)], writes=[B("ps", 7)]))
        st(lambda: K.op("dve", lambda e: e.tensor_tensor(out=Gt, in0=PBf[7][:, 0:128], in1=bgbc, op=ALU.add),
                        reads=[B("ps", 7), B("bgbc")], writes=[B("G")]))
        st(lambda: K.op("act", lambda e: e.activation(out=v3(gt["af"]), in_=fpre, func=AF.Abs),
                        reads=[B("G")], writes=[B("af")]))
        st(lambda: K.op("act", lambda e: e.activation(out=gt["ex"], in_=gt["af"], func=AF.Exp, scale=-1.0),
                        reads=[B("af")], writes=[B("ex")]))
        st(lambda: K.op("act", lambda e: e.activation(out=gt["l1"], in_=gt["ex"], func=AF.Ln, bias=1.0),
                        reads=[B("ex")], writes=[B("l1")]))
        st(lambda: K.op("dve", lambda e: e.scalar_tensor_tensor(
            out=v3(gt["lf"]), in0=fpre, scalar=0.0, in1=v3(gt["l1"]), op0=ALU.min, op1=ALU.subtract),
            reads=[B("G"), B("l1")], writes=[B("lf")]))

        def cs_mm(e):
            e.matmul(PBf[7][:, 128:192], ones_f, gt["lf"], start=True, stop=True)
            return e.matmul(PBf[7][:, 192:256], maskT, gt["lf"], start=True, stop=True)
        st_pe(lambda: K.op("pe", cs_mm, reads=[B("lf"), B("ones"), B("maskT")], writes=[B("ps", 7)]))
        for tt in range(1, NT):
            st(lambda tt=tt: K.op("dve", lambda e: e.tensor_tensor(
                out=gt["carry"][:, tt * 4:tt * 4 + 4], in0=gt["carry"][:, tt * 4 - 4:tt * 4],
                in1=PBf[7][:, 128 + tt * 4 - 4:128 + tt * 4], op=ALU.add),
                reads=[B("ps", 7), B("carry", tt - 1), B("ones")], writes=[B("carry", tt)]))
        st(lambda: K.op("dve", lambda e: e.tensor_tensor(out=gt["Bt"], in0=PBf[7][:, 192:256], in1=gt["carry"], op=ALU.add),
                        reads=[B("ps", 7)] + [B("carry", t) for t in range(1, NT)], writes=[B("Bt")]))
        st(lambda: K.op("dve", lambda e: e.tensor_tensor(out=v3(gt["a"]), in0=ipre, in1=v3(gt["Bt"]), op=ALU.subtract),
                        reads=[B("G"), B("Bt")], writes=[B("a")]))
        st_pe(lambda: K.op("pe", lambda e: e.transpose(out=PBf[7][0:64, 256:384], in_=gt["a"], identity=ident_f),
                        reads=[B("a"), B("ident_f")], writes=[B("ps", 7)]))
        st(lambda: K.op("dve", lambda e: e.reduce_max(out=cmax[0:64, 0:1], in_=PBf[7][0:64, 256:384], axis=AX.X),
                        reads=[B("ps", 7)], writes=[B("cmax")]))
        st_pe(lambda: K.op("pe", lambda e: e.transpose(out=PBf[7][0:1, 384:448], in_=cmax[0:64, 0:1],
                                                    identity=ident_f[0:64, 0:64]),
                        reads=[B("cmax"), B("ident_f")], writes=[B("ps", 7)]))
        st(lambda: K.op("dve", lambda e: e.tensor_copy(out=minc[0:1, 0:4], in_=PBf[7][0:1, 384:388]),
                        reads=[B("ps", 7)], writes=[B("minc", 0)]))
        for tt in range(1, NT):
            st(lambda tt=tt: K.op("dve", lambda e: e.tensor_tensor(
                out=minc[0:1, tt * 4:tt * 4 + 4], in0=minc[0:1, tt * 4 - 4:tt * 4],
                in1=PBf[7][0:1, 384 + tt * 4:388 + tt * 4], op=ALU.max),
                reads=[B("ps", 7), B("minc", tt - 1)], writes=[B("minc", tt)]))
        st_pe(lambda: K.op("pe", lambda e: e.matmul(PBf[7][:, 448:512], ones_f[0:1, 0:128], minc[0:1, 0:64],
                                                 start=True, stop=True),
                        reads=[B("minc", t) for t in range(NT)] + [B("ones")], writes=[B("ps", 7)]))
        st(lambda: K.op("dve", lambda e: e.tensor_copy(out=gt["Mbc"], in_=PBf[7][:, 448:512]),
                        reads=[B("ps", 7)], writes=[B("Mbc")]))
        st(lambda: K.op("dve", lambda e: e.tensor_tensor(out=gt["d"], in0=gt["a"], in1=gt["Mbc"], op=ALU.subtract),
                        reads=[B("a"), B("Mbc")], writes=[B("d")]))
        st(lambda: K.op("act", lambda e: e.activation(out=gt["e"], in_=gt["d"], func=AF.Exp),
                        reads=[B("d")], writes=[B("e")]))
        st(lambda: K.op("dve", lambda e: e.tensor_tensor(out=gt["d"][:, 0:60], in0=gt["a"][:, 0:60],
                                                         in1=gt["Mbc"][:, 4:64], op=ALU.subtract),
                        reads=[B("a"), B("Mbc")], writes=[B("d")]))
        st(lambda: K.op("act", lambda e: e.activation(out=gt["e2"][:, 0:60], in_=gt["d"][:, 0:60], func=AF.Exp),
                        reads=[B("d")], writes=[B("e2")]))
        st(lambda: K.op("dve", lambda e: e.tensor_tensor(out=gt["d"][:, 4:64], in0=gt["Mbc"][:, 0:60],
                                                         in1=gt["Mbc"][:, 4:64], op=ALU.subtract),
                        reads=[B("Mbc")], writes=[B("d")]))
        st(lambda: K.op("act", lambda e: e.activation(out=gt["r"][:, 4:64], in_=gt["d"][:, 4:64], func=AF.Exp),
                        reads=[B("d")], writes=[B("r")]))
        st(lambda: K.op("dve", lambda e: e.tensor_tensor(out=gt["d"], in0=gt["Bt"], in1=gt["Mbc"], op=ALU.add),
                        reads=[B("Bt"), B("Mbc")], writes=[B("d")]))
        st(lambda: K.op("act", lambda e: e.activation(out=gt["fl"], in_=gt["d"], func=AF.Exp, scale=-1.0),
                        reads=[B("d")], writes=[B("fl")]))

        def run_steps(n):
            for _ in range(n):
                if steps:
                    f_ = steps.pop(0)
                    if f_ is not None:
                        f_()

        items = [(cc, tb) for cc in range(8) for tb in range(NB)]

        def conv_load(cc):
            load_w(cc % 4, [(0, C_GA + cc * 128, 128), (128, C_GG + cc * 128, 128)], "glu")

        def build_wg(cc):
            s = cc % 2
            in0 = bass.AP(Rrep.tensor, Rrep.offset, [Rrep.ap[0], [0, 32], [1, 32]])
            wc = wcolT[:, cc * 32:(cc + 1) * 32]
            in1 = bass.AP(wc.tensor, wc.offset, [wc.ap[0], [1, 32], [0, 32]])
            K.op("pool", lambda e: e.tensor_tensor(out=Wgp[s], in0=in0, in1=in1, op=ALU.mult),
                 reads=[B("Rrep"), B("wcolT")], writes=[B("Wg", s)])

        conv_load(1)
        conv_load(2)
        y_fence = [(k, v) for k, v in K.cnt.items() if v > 0]

        def c_pad(e):
            for i in range(2):
                e.memset(cbuf[i][:, 0:30], 0.0)
                ins = e.memset(cbuf[i][:, 2078:2080], 0.0)
            return ins
        K.op("pool", c_pad, writes=[B("cpad")])
        build_wg(0)
        build_wg(1)

        def proj(n):
            cc, tb = items[n]
            if tb == 0:
                if cc + 2 < 8:
                    conv_load(cc + 2)
            s = n % 2
            ws = wslot[cc % 4]
            K.op("pe", lambda e: mm_group(e, PBf[s], [(ws[:, dc, 0:128], uT[:, dc, tb * 512:(tb + 1) * 512])
                                                      for dc in range(8)]),
                 reads=uT_blk(tb) + [B("w", cc % 4)], writes=[B("ps", s)])
            K.op("pe", lambda e: mm_group(e, PBf[2 + s], [(ws[:, dc, 128:256], uT[:, dc, tb * 512:(tb + 1) * 512])
                                                          for dc in range(8)]),
                 reads=uT_blk(tb) + [B("w", cc % 4)], writes=[B("ps", 2 + s)])
            K.op("act", lambda e: e.activation(out=thb[s], in_=PBf[2 + s], func=AF.Tanh, scale=0.5),
                 reads=[B("ps", 2 + s)], writes=[B("th", s)])
            cs = cc % 2
            K.op("dve", lambda e: e.scalar_tensor_tensor(
                out=cbuf[cs][:, 30 + tb * 512:30 + (tb + 1) * 512], in0=thb[s], scalar=1.0, in1=PBf[s],
                op0=ALU.add, op1=ALU.mult),
                reads=[B("th", s), B("ps", s)], writes=[B("cbuf", cs, tb)])

        def replicate(cc):
            cs = cc % 2
            rd = [B("cbuf", cs, tb) for tb in range(NB)] + [B("cpad")]
            for q in range(4):
                for j in range(4):
                    K.dma("pool" if q == 3 else "sp", lambda e, q=q, j=j: e.dma_start(
                        out=Xt[q][32 * j:32 * j + 32, 0:2076], in_=cbuf[cs][32 * q:32 * q + 32, j:j + 2076]),
                        reads=rd, writes=[B("Xt", q, j)])

        def conv(n):
            cc, tb = items[n]
            s = n % 2
            cs = cc % 2
            wg3 = Wgp[cs]

            def cmm(e):
                ins = None
                for g in range(8):
                    for q in range(4):
                        ins = e.matmul(PBf[4 + s][32 * q:32 * q + 32, :], wg3[:, q * 8 + g, :],
                                       Xt[q][:, tb * 512 + 4 * g:tb * 512 + 4 * g + 512],
                                       start=(g == 0), stop=(g == 7), tile_position=(0, 32 * q))
                return ins
            K.op("pe", cmm, reads=[B("Xt", q, j) for q in range(4) for j in range(4)] + [B("Wg", cs)],
                 writes=[B("ps", 4 + s)])
            K.op("act", lambda e: e.activation(out=ysl(cc, tb), in_=PBf[4 + s], func=AF.Identity,
                                               bias=convbT[:, cc:cc + 1]),
                 reads=[B("ps", 4 + s), B("convbT")], writes=[B("y", cc, tb)], extra=y_fence)

        def zc_load(cc):
            load_w(cc % 4, [(0, C_ZC + cc * 128, 128)], "zc")

        sctr = [0]

        def stat_item(tb, cc):
            ipm, ipq = tb * 2, tb * 2 + 1
            s = sctr[0] % 3
            s2 = sctr[0] % 2
            sctr[0] += 1
            K.op("dve", lambda e: e.tensor_copy(out=ybb[s], in_=ysl(cc, tb)),
                 reads=[B("y", cc, tb)], writes=[B("yb", s)])
            K.op("act", lambda e: e.activation(out=ysqb[s2], in_=ysl(cc, tb), func=AF.Square),
                 reads=[B("y", cc, tb)], writes=[B("ysq", s2)])

            def stat(e):
                e.matmul(PBf[ipm], onesb, ybb[s], start=(cc == 0), stop=(cc == 7))
                return e.matmul(PBf[ipq], onesb, ysqb[s2], start=(cc == 0), stop=(cc == 7))
            K.op("pe", stat, reads=[B("yb", s), B("ysq", s2), B("ones")],
                 writes=[B("ps", ipm), B("ps", ipq)])

        for cc in range(9):
            if 1 <= cc < 8:
                for tb in range(NB):
                    proj(cc * 4 + tb)
                    run_steps(3)
            if cc == 8:
                run_steps(1000)
                zc_load(0)
                zc_load(1)
                for tb in (0, 1, 3):
                    for c2 in range(7):
                        stat_item(tb, c2)
            if cc >= 1:
                for tb in range(NB):
                    conv((cc - 1) * 4 + tb)
                if cc + 1 < 8:
                    build_wg(cc + 1)
            if cc < 8:
                replicate(cc)
        for tb in (0, 1, 3):
            stat_item(tb, 7)
        for c2 in range(8):
            stat_item(2, c2)
        def _sl(tb):
            return slice(tb * 512, (tb + 1) * 512)
        for tb in range(NB):
            K.op("dve", lambda e, tb=tb: e.tensor_copy(out=mu[:, _sl(tb)], in_=PBf[tb * 2]),
                 reads=[B("ps", tb * 2)], writes=[B("mu", tb)])
            K.op("act", lambda e, tb=tb: e.activation(out=var[:, _sl(tb)], in_=PBf[tb * 2], func=AF.Square),
                 reads=[B("ps", tb * 2)], writes=[B("var", tb)])
        for tb in range(NB):
            K.op("dve", lambda e, tb=tb: e.tensor_tensor(out=var[:, _sl(tb)], in0=PBf[tb * 2 + 1], in1=var[:, _sl(tb)],
                                                        op=ALU.subtract),
                 reads=[B("ps", tb * 2 + 1), B("var", tb)], writes=[B("var", tb)])
        for tb in range(NB):
            K.op("act", lambda e, tb=tb: e.activation(out=var[:, _sl(tb)], in_=var[:, _sl(tb)], func=AF.Ln, bias=EPS),
                 reads=[B("var", tb)], writes=[B("var", tb)])
            K.op("act", lambda e, tb=tb: e.activation(out=var[:, _sl(tb)], in_=var[:, _sl(tb)], func=AF.Exp, scale=-0.5),
                 reads=[B("var", tb)], writes=[B("var", tb)])

        def head_loads(h, part=None):
            ozs = (4, 5) if h % 2 == 0 else (6, 3)
            jobs = [(0, C_Q, "q"), (1, C_K, "k"), (2, C_V, "v"), (ozs[0], C_O, "o"), (ozs[1], C_ZM, "z")]
            for k, (slot, col, nm) in enumerate(jobs):
                if part is None or part == k:
                    load_w(slot, [(0, col + h * 256, 256)], nm)
        nitems = [(cc, tb) for cc in range(8) for tb in range(NB)]

        def n_t1(n):
            cc, tb = nitems[n]
            if tb == 0 and cc + 2 < 8:
                zc_load(cc + 2)
            s = n % 4
            sl = slice(tb * 512, (tb + 1) * 512)
            K.op("dve", lambda e: e.tensor_tensor(out=tnb[s], in0=ysl(cc, tb), in1=mu[:, sl], op=ALU.subtract),
                 reads=[B("y", cc, tb), B("mu", tb)], writes=[B("tn_", s)])
            eng = "pool" if n % 2 == 0 else "dve"
            K.op(eng, lambda e: e.tensor_tensor(out=tnb[s], in0=tnb[s], in1=var[:, sl], op=ALU.mult),
                 reads=[B("tn_", s), B("var", tb)], writes=[B("tn_", s)])

        def n_mid(n):
            cc, tb = nitems[n]
            s = n % 4
            p = n % 2
            ws = wslot[cc % 4]
            K.op("pe", lambda e: mm_group(
                e, PBf[4 + p], [(ws[:, dc, 0:128], uT[:, dc, tb * 512:(tb + 1) * 512]) for dc in range(8)]),
                reads=uT_blk(tb) + [B("w", cc % 4)], writes=[B("ps", 4 + p)])
            K.op("act", lambda e: e.activation(out=szb[n % 3], in_=PBf[4 + p], func=AF.Silu),
                 reads=[B("ps", 4 + p)], writes=[B("sz", n % 3)])
            K.op("act", lambda e: e.activation(out=tnb[s], in_=tnb[s], func=AF.Silu,
                                               scale=lngT[:, cc:cc + 1], bias=lnbT[:, cc:cc + 1]),
                 reads=[B("tn_", s), B("lngT"), B("lnbT")], writes=[B("tn_", s)])

        def n_fin(n):
            cc, tb = nitems[n]
            s = n % 4
            sl = slice(tb * 512, (tb + 1) * 512)
            K.op("dve", lambda e: e.tensor_tensor(out=mixT[:, 8 + cc, sl], in0=tnb[s], in1=szb[n % 3], op=ALU.mult),
                 reads=[B("tn_", s), B("sz", n % 3)], writes=[B("mix", 8 + cc, tb)])
        for i in range(len(nitems) + 2):
            hl = {10: 3, 14: 4, 22: 0, 26: 1, 30: 2}
            if i in hl:
                head_loads(0, part=hl[i])
            if i < len(nitems):
                n_t1(i)
            if 0 <= i - 1 < len(nitems):
                n_mid(i - 1)
            if 0 <= i - 2 < len(nitems):
                n_fin(i - 2)
        K.barrier()

        for ec in range(16):
            K.dma("pool", lambda e, ec=ec: e.dma_start(out=woutT[:, ec, :], in_=w_out[ec * 128:(ec + 1) * 128, :]),
                  writes=[B("wout", ec)])
        K.op("pool", lambda e: e.memset(vext[:, :, 256:257], 1.0), writes=[B("vone")])
        for ec in range(8):
            K.op("pool", lambda e, ec=ec: e.tensor_scalar(
                out=woutT[:, ec, :], in0=woutT[:, ec, :], scalar1=mhgT[:, ec:ec + 1], scalar2=1.0,
                op0=ALU.mult, op1=ALU.mult),
                reads=[B("wout", ec), B("mhgT")], writes=[B("wout", ec)])
        e_, e2_, r_, fl_ = gt["e"], gt["e2"], gt["r"], gt["fl"]
        gate_bufs = [B("e"), B("e2"), B("r"), B("fl")]

        pnc = [0]

        tails = []

        def run_tail():
            if tails:
                tails.pop(0)()

        def head_body(h):
            pn = pnc[0]
            oz = (4, 5) if h % 2 == 0 else (6, 3)
            wo, wz = wslot[oz[0]], wslot[oz[1]]
            for which in range(2):
                dst = qT if which == 0 else kT
                nm = "qT" if which == 0 else "kT"
                ws = wslot[which]
                scl = 1.0 if which == 0 else 1.0 / 16.0
                for half in range(2):
                    for tb in range(NB):
                        pb = pn % 4
                        pn += 1
                        sl = slice(tb * 512, (tb + 1) * 512)
                        K.op("pe", lambda e, ws=ws, half=half, tb=tb, pb=pb: mm_group(
                            e, PBf[pb], [(ws[:, dc, half * 128:(half + 1) * 128], uT[:, dc, tb * 512:(tb + 1) * 512])
                                         for dc in range(8)]),
                            reads=uT_blk(tb) + [B("w", which)], writes=[B("ps", pb)])
                        if pn % 2 == 0:
                            K.op("act", lambda e, dst=dst, half=half, sl=sl, pb=pb, scl=scl: e.activation(
                                out=dst[:, half, sl], in_=PBf[pb], func=AF.Copy, scale=scl),
                                reads=[B("ps", pb)], writes=[B(nm, half, tb)])
                        else:
                            K.op("dve", lambda e, dst=dst, half=half, sl=sl, pb=pb, scl=scl: e.tensor_scalar(
                                out=dst[:, half, sl], in0=PBf[pb], scalar1=scl, scalar2=None, op0=ALU.mult),
                                reads=[B("ps", pb)], writes=[B(nm, half, tb)])
                        if which * 8 + half * 4 + tb + 1 in (5, 9):
                            run_tail()
            for tt in range(NT):
                pb = pn % 4
                pn += 1
                K.op("pe", lambda e, tt=tt, pb=pb: mm_group(
                    e, PBf[pb][:, 0:256], [(uT[:, dc, tt * 128:(tt + 1) * 128], wslot[2][:, dc, 0:256])
                                           for dc in range(8)]),
                    reads=[B("uT", tt), B("w", 2)], writes=[B("ps", pb)])
                if pn % 2 == 0:
                    K.op("act", lambda e, tt=tt, pb=pb: e.activation(out=vext[:, tt, 0:256], in_=PBf[pb][:, 0:256],
                                                                     func=AF.Copy),
                         reads=[B("ps", pb)], writes=[B("v", tt)])
                else:
                    K.op("dve", lambda e, tt=tt, pb=pb: e.tensor_copy(out=vext[:, tt, 0:256], in_=PBf[pb][:, 0:256]),
                         reads=[B("ps", pb)], writes=[B("v", tt)])
            def gcol_(c):
                return slice(c * 4 + h, c * 4 + h + 1)

            def stage_A(c):
                s = c % 2
                ct = slice(c * 128, (c + 1) * 128)
                tbc = c // 4
                gcol = gcol_(c)

                def ozmm(e):
                    mm_group(e, PBf[s][:, 0:256], [(uT[:, dc, ct], wo[:, dc, 0:256]) for dc in range(8)])
                    return mm_group(e, PBf[s][:, 256:512], [(uT[:, dc, ct], wz[:, dc, 0:256]) for dc in range(8)])
                K.op("pe", ozmm, reads=[B("uT", c), B("w", oz[0]), B("w", oz[1])], writes=[B("ps", s)])
                K.op("act", lambda e: e.activation(out=tho[s], in_=PBf[s][:, 0:256], func=AF.Tanh, scale=0.5),
                     reads=[B("ps", s)], writes=[B("tho", s)])
                s3 = c % 3
                K.op("act", lambda e: e.activation(out=szm[s3], in_=PBf[s][:, 256:512], func=AF.Silu),
                     reads=[B("ps", s)], writes=[B("szm", s3)])
                K.op("pe", lambda e: mm_group(
                    e, PBf[2 + s][:, 384:512], [(kT[:, half, ct], qT[:, half, ct]) for half in range(2)]),
                    reads=[B("kT", 0, tbc), B("kT", 1, tbc), B("qT", 0, tbc), B("qT", 1, tbc)],
                    writes=[B("ps", 2 + s)])
                K.op("dve", lambda e: e.scalar_tensor_tensor(
                    out=wTb[s], in0=PBf[2 + s][:, 384:512], scalar=e_[:, gcol], in1=maskT,
                    op0=ALU.mult, op1=ALU.mult),
                    reads=[B("ps", 2 + s), B("e"), B("maskT")], writes=[B("wT", s)])
                if c < NT - 1:
                    def ktr(e):
                        ins = None
                        for half in range(2):
                            ins = e.transpose(out=PBb[6][:, half * 128:(half + 1) * 128],
                                              in_=kT[:, half, ct], identity=ident_bf)
                        return ins
                    K.op("pe", ktr, reads=[B("kT", 0, tbc), B("kT", 1, tbc), B("ident_bf")], writes=[B("ps", 6)])
                    K.op("act", lambda e: e.activation(
                        out=ekb[s], in_=PBb[6][:, 0:256], func=AF.Copy, scale=e2_[:, gcol]),
                        reads=[B("ps", 6), B("e2")], writes=[B("ek", s)])

            def stage_O3(c):
                s = c % 2
                cc1 = slice(c, c + 1)
                K.op("dve", lambda e: e.scalar_tensor_tensor(
                    out=hmb[s], in0=Ab[s], scalar=tiny[6][:, cc1], in1=szm[c % 3], op0=ALU.mult, op1=ALU.mult),
                    reads=[B("A", s), B("tn", 6, c), B("szm", c % 3)], writes=[B("hm", s)])

            def stage_B(c):
                s = c % 2
                ct = slice(c * 128, (c + 1) * 128)
                tbc = c // 4
                gcol = gcol_(c)
                cc1 = slice(c, c + 1)
                cb = c % 2
                pairs = [(wTb[s], vext[:, c, 0:257])]
                rds = [B("wT", s), B("v", c), B("vone")]
                if c > 0:
                    pairs += [(qT[:, half, ct], Cbf[cb][:, half, 0:257]) for half in range(2)]
                    rds += [B("qT", 0, tbc), B("qT", 1, tbc), B("Cbf", cb, 0), B("Cbf", cb, 1)]
                if c < NT - 1:
                    def kvmm(e):
                        e.matmul(PBf[4][:, 0:257], ekb[s][:, 0:128], vext[:, c, 0:257], start=True, stop=True)
                        return e.matmul(PBf[5][:, 0:257], ekb[s][:, 128:256], vext[:, c, 0:257], start=True, stop=True)
                    K.op("pe", kvmm, reads=[B("ek", s), B("v", c), B("vone")], writes=[B("ps", 4), B("ps", 5)])
                K.op("pe", lambda e: mm_group(e, PBf[2 + s][:, 0:257], pairs), reads=rds, writes=[B("ps", 2 + s)])
                if c < NT - 1:
                    rcol = slice((c + 1) * 4 + h, (c + 1) * 4 + h + 1)
                    nb = (c + 1) % 2
                    for half in range(2):
                        if c == 0:
                            K.op("dve", lambda e, half=half: e.tensor_copy(out=Cm[:, half, 0:257],
                                                                           in_=PBf[4 + half][:, 0:257]),
                                 reads=[B("ps", 4 + half)], writes=[B("Cm", half)])
                        else:
                            K.op("dve", lambda e, half=half: e.scalar_tensor_tensor(
                                out=Cm[:, half, 0:257], in0=Cm[:, half, 0:257], scalar=r_[:, rcol],
                                in1=PBf[4 + half][:, 0:257], op0=ALU.mult, op1=ALU.add),
                                reads=[B("ps", 4 + half), B("Cm", half), B("r")], writes=[B("Cm", half)])
                        K.op("act", lambda e, half=half: e.activation(out=Cbf[nb][:, half, 0:257],
                                                                      in_=Cm[:, half, 0:257], func=AF.Copy),
                             reads=[B("Cm", half)], writes=[B("Cbf", nb, half)])
                K.op("dve", lambda e: e.tensor_scalar(
                    out=tiny[7][:, cc1], in0=PBf[2 + s][:, 256:257], scalar1=fl_[:, gcol], scalar2=None,
                    op0=ALU.max),
                    reads=[B("ps", 2 + s), B("fl")], writes=[B("tn", 7, c)])
                K.op("dve", lambda e: e.scalar_tensor_tensor(
                    out=tiny[0][:, cc1], in0=PBf[2 + s][:, 256:257], scalar=-1.0, in1=tiny[7][:, cc1],
                    op0=ALU.mult, op1=ALU.max),
                    reads=[B("ps", 2 + s), B("tn", 7, c)], writes=[B("tn", 0, c)])
                K.op("dve", lambda e: e.reciprocal(out=tiny[1][:, cc1], in_=tiny[0][:, cc1]),
                     reads=[B("tn", 0, c)], writes=[B("tn", 1, c)])
                K.op("dve", lambda e: e.scalar_tensor_tensor(
                    out=Ab[s], in0=tho[s], scalar=1.0, in1=PBf[2 + s][:, 0:256], op0=ALU.add, op1=ALU.mult),
                    reads=[B("tho", s), B("ps", 2 + s)], writes=[B("A", s)])
                K.op("act", lambda e: e.activation(out=junkm, in_=Ab[s], func=AF.Square,
                                                   accum_out=tiny[2][:, cc1]),
                     reads=[B("A", s)], writes=[B("junkm"), B("tn", 2, c)])

                K.op("pool", lambda e: e.tensor_scalar(
                    out=tiny[3][:, cc1], in0=tiny[2][:, cc1], scalar1=tiny[1][:, cc1],
                    scalar2=tiny[1][:, cc1], op0=ALU.mult, op1=ALU.mult),
                    reads=[B("tn", 2, c), B("tn", 1, c)], writes=[B("tn", 3, c)])
                K.op("pool", lambda e: e.tensor_scalar(
                    out=tiny[4][:, cc1], in0=tiny[3][:, cc1], scalar1=0.25 / 256.0, scalar2=EPS,
                    op0=ALU.mult, op1=ALU.add),
                    reads=[B("tn", 3, c)], writes=[B("tn", 4, c)])
                K.op("pool", lambda e: e.tensor_tensor(out=tiny[5][:, cc1], in0=tiny[4][:, cc1], in1=nhalf[:, 0:1],
                                                       op=ALU.pow),
                     reads=[B("tn", 4, c), B("ones")], writes=[B("tn", 5, c)])
                K.op("pool", lambda e: e.tensor_scalar(
                    out=tiny[6][:, cc1], in0=tiny[5][:, cc1], scalar1=tiny[1][:, cc1],
                    scalar2=0.5, op0=ALU.mult, op1=ALU.mult),
                    reads=[B("tn", 5, c), B("tn", 1, c)], writes=[B("tn", 6, c)])

            def stage_C(c):
                s = c % 2
                ct = slice(c * 128, (c + 1) * 128)

                def htr(e):
                    ins = None
                    for half in range(2):
                        ins = e.transpose(out=PBb[7][:, half * 128:(half + 1) * 128],
                                          in_=hmb[s][:, half * 128:(half + 1) * 128], identity=ident_bf)
                    return ins
                K.op("pe", htr, reads=[B("hm", s), B("ident_bf")], writes=[B("ps", 7)])

                K.op("act", lambda e: e.activation(
                    out=mixT[:, h * 2:h * 2 + 2, ct], in_=PBb[7][:, 0:256].rearrange("p (a b) -> p a b", a=2),
                    func=AF.Copy),
                    reads=[B("ps", 7)], writes=[B("mix", h * 2, c), B("mix", h * 2 + 1, c)])

            def step(i):
                if 2 <= i < 7 and h + 1 < 4:
                    head_loads(h + 1, part=i - 2)
                if 0 <= i + 1 < NT:
                    stage_A(i + 1)
                if 0 <= i < NT:
                    stage_B(i)
                if 0 <= i - 1 < NT:
                    stage_O3(i - 1)
                if 0 <= i - 2 < NT:
                    stage_C(i - 2)
            for i in range(-1, NT):
                step(i)
            tails.append(lambda: step(NT))
            tails.append(lambda: step(NT + 1))
            pnc[0] = pn
        for h in range(4):
            head_body(h)
        while tails:
            run_tail()
        K.barrier()

        K.dma("sp", lambda e: e.dma_start(out=gbc, in_=bc_row(fin_g, D)), writes=[B("gbc")])
        def x_load4(tt):
            K.dma("sp", lambda e, tt=tt: e.dma_start(out=xs[tt % 3], in_=x[tt * 128:(tt + 1) * 128, :]),
                  writes=[B("xs", tt % 3)])
        x_load4(0)
        x_load4(1)
        for tt in range(NT):
            s3, s2 = tt % 3, tt % 2
            ct = slice(tt * 128, (tt + 1) * 128)
            if tt + 2 < NT:
                x_load4(tt + 2)
            for nh in range(2):
                pb = s2 * 2 + nh
                K.op("pe", lambda e, pb=pb, ct=ct, nh=nh: mm_group(
                    e, PBf[pb], [(mixT[:, ec, ct], woutT[:, ec, nh * 512:(nh + 1) * 512]) for ec in range(16)]),
                    reads=[B("wout", ec) for ec in range(16)], writes=[B("ps", pb)])
                K.op("dve", lambda e, pb=pb, s2=s2, s3=s3, nh=nh: e.tensor_tensor(
                    out=hres[s2][:, nh * 512:(nh + 1) * 512], in0=PBf[pb], in1=xs[s3][:, nh * 512:(nh + 1) * 512],
                    op=ALU.add),
                    reads=[B("ps", pb), B("xs", s3)], writes=[B("hres", s2, nh)])
            K.op("act", lambda e, s2=s2, tt=tt: e.activation(out=junk4, in_=hres[s2], func=AF.Square,
                                                             accum_out=ssq4[:, tt:tt + 1]),
                 reads=[B("hres", s2, 0), B("hres", s2, 1)], writes=[B("junk4"), B("ssq4", tt)])
            K.op("act", lambda e, tt=tt: e.activation(out=srt4[:, tt:tt + 1], in_=ssq4[:, tt:tt + 1],
                                                      func=AF.Sqrt, scale=1.0 / D, bias=EPS),
                 reads=[B("ssq4", tt)], writes=[B("srt4", tt)])
            K.op("dve", lambda e, tt=tt: e.reciprocal(out=rstd4[:, tt:tt + 1], in_=srt4[:, tt:tt + 1]),
                 reads=[B("srt4", tt)], writes=[B("rstd4", tt)])
            K.op("dve", lambda e, s2=s2, tt=tt: e.scalar_tensor_tensor(
                out=outt[s2], in0=hres[s2], scalar=rstd4[:, tt:tt + 1], in1=gbc, op0=ALU.mult, op1=ALU.mult),
                reads=[B("hres", s2, 0), B("hres", s2, 1), B("rstd4", tt), B("gbc")], writes=[B("outt", s2)])
            out_toks.append(K.dma("sp", lambda e, tt=tt, s2=s2: e.dma_start(out=out[tt * 128:(tt + 1) * 128, :],
                                                                            in_=outt[s2]),
                                  reads=[B("outt", s2)]))
        K.wait_all("sp", out_toks)
        if debug:
            K.barrier()
            dcon = nc.dram_tensor("d_const", [128, 12288], U8, kind="ExternalOutput").ap()
            dut = nc.dram_tensor("d_uT", [128, 8 * 2048], BF16, kind="ExternalOutput").ap()
            dmix = nc.dram_tensor("d_mixT", [128, 16 * 2048], BF16, kind="ExternalOutput").ap()
            t1 = K.dma("sp", lambda e: e.dma_start(out=dcon, in_=arena[:, 0:12288]))
            t2 = K.dma("sp", lambda e: e.dma_start(out=dut, in_=arena[:, R_UT:R_UT + 32768].bitcast(BF16)))
            t3 = K.dma("sp", lambda e: e.dma_start(out=dmix, in_=arena[:, R_MIX:R_MIX + 65536].bitcast(BF16)))
            K.wait_all("sp", [t1, t2, t3])
        K.flush()
    if debug:
        offs = {}
        names = ["ident_bf", "ident_f", "maskT", "ones_f", "onesb", "convbT", "lngT", "lnbT", "mhgT", "nhalf", "Rrep",
                 "wconvT", "gbc", "Gt", "bgbc", "af", "ex", "l1", "lf", "carry", "Bt", "a", "Mbc", "e", "e2", "r",
                 "fl", "d", "cmax", "minc"]
        for nm, (off, nb) in zip(names, carve_log):
            offs[nm] = (off, nb)
        nc._dbg_offs = offs
    return nc


_NC = None


def kernel(x, norm_g, w_in, b_gates, mh_norm_g, conv_w, conv_b, conv_ln_g, conv_ln_b, w_out, final_norm_g):
    global _NC
    if _NC is None:
        _NC = build()
    nc = _NC
    f = lambda a: np.ascontiguousarray(np.asarray(a, dtype=np.float32))
    shared = {
        "norm_g": f(norm_g).reshape(1, D), "w_in": f(w_in).reshape(D, DIN), "b_gates": f(b_gates).reshape(1, 8),
        "mh_norm_g": f(mh_norm_g).reshape(1, D), "conv_w": f(conv_w).reshape(31, D),
        "conv_b": f(conv_b).reshape(1, D), "conv_ln_g": f(conv_ln_g).reshape(1, D),
        "conv_ln_b": f(conv_ln_b).reshape(1, D), "w_out": f(w_out).reshape(2 * D, D),
        "final_norm_g": f(final_norm_g).reshape(1, D),
    }
    xx = f(x)
    in_maps = [dict(shared, x=xx[b]) for b in range(8)]
    res = run_bass_kernel_spmd(nc, in_maps, core_ids=list(range(8)))
    return np.stack([np.asarray(r["out"], dtype=np.float32).reshape(S, D) for r in res.results], axis=0)
```

```python
import os
import numpy as np
import concourse.bass as bass
import concourse.mybir as mybir
from concourse.bass_utils import run_bass_kernel_spmd

F32 = mybir.dt.float32
BF16 = mybir.dt.bfloat16
U8 = mybir.dt.uint8
ALU = mybir.AluOpType
AF = mybir.ActivationFunctionType
AX = mybir.AxisListType

S = 2048
D = 1024
NT = 16
NB = 4
DIN = 8200
EPS = 1e-6
C_Q, C_K, C_V, C_O, C_ZM, C_I, C_GA, C_GG, C_ZC = 0, 1024, 2048, 3072, 4096, 5120, 5128, 6152, 7176


class Buf:
    __slots__ = ("lw", "rd", "excl")

    def __init__(self, excl=False):
        self.lw = None
        self.rd = {}
        self.excl = excl


class Sched:
    def __init__(self, nc, block):
        self.nc = nc
        self.fn = {"pe": block.tensor, "act": block.scalar, "dve": block.vector,
                   "pool": block.gpsimd, "sp": block.sync}
        self.semh = {}
        self.cnt = {}
        for e in ("pe", "act", "dve", "pool"):
            self.semh[e] = nc.alloc_semaphore("s_" + e)
            self.cnt[e] = 0
        self.waited = {e: {} for e in self.fn}
        self.dq = {}
        for q, n in (("sp", 8), ("pool", 8), ("act", 4)):
            lst = []
            for i in range(n):
                key = "d_%s%d" % (q, i)
                self.semh[key] = nc.alloc_semaphore(key)
                self.cnt[key] = 0
                lst.append(key)
            self.dq[q] = [lst, 0]
        self.bufs = {}
        self.q = {e: [] for e in self.fn}

    def flush(self):
        for eng, lst in self.q.items():
            if lst:
                def run_all(e, lst=lst):
                    for f in lst:
                        f(e)
                self.fn[eng](run_all)
            self.q[eng] = []

    def B(self, *name):
        b = self.bufs.get(name)
        if b is None:
            b = self.bufs[name] = Buf(excl=(name[0] == "ps"))
        return b

    def _deps(self, eng, reads, writes, extra):
        deps = {}

        def add(tok):
            if tok is None:
                return
            k, v = tok
            if v > deps.get(k, 0):
                deps[k] = v
        xr = [b for b in reads if b.excl]
        for b in reads:
            add(b.lw)
        for b in list(writes) + xr:
            add(b.lw)
            for k, v in b.rd.items():
                add((k, v))
        for t in extra:
            add(t)
        waits = []
        w = self.waited[eng]
        for k, v in deps.items():
            if eng == "pe" and k == "pe":
                continue
            if w.get(k, 0) >= v:
                continue
            w[k] = v
            waits.append((self.semh[k], v))
        return waits

    def _mark(self, tok, reads, writes):
        k, v = tok
        for b in reads:
            if b.excl:
                continue
            if v > b.rd.get(k, 0):
                b.rd[k] = v
        for b in list(writes) + [b for b in reads if b.excl]:
            b.lw = tok
            b.rd = {}

    def op(self, eng, fn, reads=(), writes=(), extra=()):
        waits = self._deps(eng, reads, writes, extra)
        self.cnt[eng] += 1
        tok = (eng, self.cnt[eng])
        sem = self.semh[eng]

        def run(e):
            for s, v in waits:
                e.wait_ge(s, v)
            fn(e).then_inc(sem, 1)
        self.q[eng].append(run)
        self._mark(tok, reads, writes)
        return tok

    def dma(self, q, fn, reads=(), writes=(), extra=()):
        lst, idx = self.dq[q]
        key = lst[idx % len(lst)]
        self.dq[q][1] = idx + 1
        prior = self.cnt[key]
        ex = list(extra)
        if prior:
            ex.append((key, prior))
        waits = self._deps(q, reads, writes, ex)
        self.cnt[key] = prior + 16
        tok = (key, prior + 16)
        sem = self.semh[key]

        def run(e):
            for s, v in waits:
                e.wait_ge(s, v)
            fn(e).then_inc(sem, 16)
        self.q[q].append(run)
        self._mark(tok, reads, writes)
        return tok

    def barrier(self):
        cur = [(k, v) for k, v in self.cnt.items() if v > 0]
        for eng in self.fn:
            waits = []
            w = self.waited[eng]
            for k, v in cur:
                if w.get(k, 0) >= v:
                    continue
                w[k] = v
                waits.append((self.semh[k], v))
            if waits:
                def run(e, waits=waits):
                    for s, v in waits:
                        e.wait_ge(s, v)
                self.q[eng].append(run)

    def wait_all(self, eng, toks):
        waits = []
        w = self.waited[eng]
        for k, v in toks:
            if w.get(k, 0) >= v:
                continue
            w[k] = v
            waits.append((self.semh[k], v))

        def run(e):
            for s, v in waits:
                e.wait_ge(s, v)
        if waits:
            self.q[eng].append(run)


def mm_group(e, out, pairs):
    n = len(pairs)
    ins = None
    for i, (l, r) in enumerate(pairs):
        ins = e.matmul(out, l, r, start=(i == 0), stop=(i == n - 1))
    return ins


def build(debug=()):
    nc = bass.Bass("TRN2", target_bir_lowering=False)
    x = nc.dram_tensor("x", [S, D], F32, kind="ExternalInput").ap()
    norm_g = nc.dram_tensor("norm_g", [1, D], F32, kind="ExternalInput").ap()
    w_in = nc.dram_tensor("w_in", [D, DIN], F32, kind="ExternalInput").ap()
    b_gates = nc.dram_tensor("b_gates", [1, 8], F32, kind="ExternalInput").ap()
    mh_g = nc.dram_tensor("mh_norm_g", [1, D], F32, kind="ExternalInput").ap()
    conv_w = nc.dram_tensor("conv_w", [31, D], F32, kind="ExternalInput").ap()
    conv_b = nc.dram_tensor("conv_b", [1, D], F32, kind="ExternalInput").ap()
    ln_g = nc.dram_tensor("conv_ln_g", [1, D], F32, kind="ExternalInput").ap()
    ln_b = nc.dram_tensor("conv_ln_b", [1, D], F32, kind="ExternalInput").ap()
    w_out = nc.dram_tensor("w_out", [2 * D, D], F32, kind="ExternalInput").ap()
    fin_g = nc.dram_tensor("final_norm_g", [1, D], F32, kind="ExternalInput").ap()
    out = nc.dram_tensor("out", [S, D], F32, kind="ExternalOutput").ap()
    dbg_out = {}
    carve_log = []

    ARENA = 12288 + 32768 + 65536 + 28672 + 32768 + 39552
    arena = nc.alloc_sbuf_tensor("arena", [128, ARENA], U8).ap()

    def carve(off, nbytes, dt, shape=None):
        ap = arena[:, off:off + nbytes].bitcast(dt)
        carve_log.append((off, nbytes))
        if shape is not None and len(shape) == 2:
            ap = ap.rearrange("p (a b) -> p a b", a=shape[0])
        elif shape is not None and len(shape) == 3:
            ap = ap.rearrange("p (a b c) -> p a b c", a=shape[0], b=shape[1])
        return ap

    o = 0
    ident_bf = carve(o, 256, BF16); o += 256
    ident_f = carve(o, 512, F32); o += 512
    maskT = carve(o, 512, F32); o += 512
    ones_f = carve(o, 512, F32); o += 512
    onesb = carve(o, 256, BF16); o += 256
    convbT = carve(o, 32, F32); o += 32
    lngT = carve(o, 32, F32); o += 32
    lnbT = carve(o, 32, F32); o += 32
    mhgT = carve(o, 32, F32); o += 32
    nhalf = carve(o, 32, F32); o += 32
    Rrep = carve(o, 64, BF16); o += 64
    o += 32
    wconvT = carve(o, 1024, F32, (8, 32)); o += 1024
    gbc = carve(o, 4096, F32); o += 4096
    Gt = carve(o, 512, F32); o += 512
    bgbc = carve(o, 512, F32); o += 512
    gt = {}
    for nm in ("af", "ex", "l1", "lf", "carry", "Bt", "a", "Mbc", "e", "e2", "r", "fl", "d"):
        gt[nm] = carve(o, 256, F32); o += 256
    cmax = carve(o, 32, F32); o += 32
    minc = carve(o, 256, F32); o += 256
    o = (o + 255) // 256 * 256
    assert o <= 12288, o
    R_UT = 12288
    R_MIX = R_UT + 32768
    R_W = R_MIX + 65536
    R_Y = R_W + 28672
    R_C = R_Y + 32768
    assert R_C + 39552 <= ARENA

    uT = carve(R_UT, 32768, BF16, (8, 2048))
    mixT = carve(R_MIX, 65536, BF16, (16, 2048))
    y_lo = carve(R_MIX, 32768, F32, (4, 2048))
    y_hi = carve(R_Y, 32768, F32, (4, 2048))
    wslot = [carve(R_W + i * 4096, 4096, BF16, (8, 256)) for i in range(7)]
    woutT = carve(R_Y, 32768, BF16, (16, 1024))

    def ysl(cc, tb):
        t = y_lo if cc < 4 else y_hi
        return t[:, cc % 4, tb * 512:(tb + 1) * 512]

    xs = [carve(R_C + i * 4096, 4096, F32) for i in range(3)]
    xs0 = [carve(R_Y + i * 4096, 4096, F32) for i in range(5)]
    xn = [carve(R_Y + 20480 + i * 2048, 2048, BF16) for i in range(3)]
    junk = carve(R_Y + 26624, 2048, BF16)
    ssq = carve(R_Y + 28672, 64, F32)
    srt = carve(R_Y + 28736, 64, F32)
    rstd0 = carve(R_Y + 28800, 64, F32)
    cw = carve(R_C + 33152, 4096, F32)
    hres = [carve(R_C + 12288 + i * 4096, 4096, F32) for i in range(2)]
    outt = [carve(R_C + 20480 + i * 4096, 4096, F32) for i in range(2)]
    junk4 = carve(R_C + 28672, 2048, BF16)
    ssq4 = carve(R_C + 30720, 64, F32)
    srt4 = carve(R_C + 30784, 64, F32)
    rstd4 = carve(R_C + 30848, 64, F32)
    Xt = [carve(R_C + i * 4160, 4160, BF16) for i in range(4)]
    Wgp = [carve(R_C + 29056 + i * 2048, 2048, BF16, (32, 32)) for i in range(2)]
    wcolT = carve(R_C + 38528, 1024, F32)
    Pq = [carve(R_MIX + i * 896, 896, F32) for i in range(4)]
    cbuf = [carve(R_C + 16640 + i * 4160, 4160, BF16) for i in range(2)]
    thb = [carve(R_C + 24960 + i * 2048, 2048, F32) for i in range(2)]
    ybb = [carve(R_C + 33152 + i * 1024, 1024, BF16) for i in range(3)]
    ysqb = [carve(R_C + 36224 + i * 1024, 1024, BF16) for i in range(2)]
    mu = carve(R_C, 8192, F32)
    var = carve(R_C + 8192, 8192, F32)
    tnb = [carve(R_C + 16384 + i * 2048, 2048, F32) for i in range(4)]
    szb = [carve(R_C + 24576 + i * 2048, 2048, F32) for i in range(3)]
    qT = carve(R_C, 8192, BF16, (2, 2048))
    kT = carve(R_C + 8192, 8192, BF16, (2, 2048))
    vext = carve(R_C + 16384, 8256, BF16, (16, 258))
    Cm = carve(R_C + 24640, 2080, F32, (2, 260))
    Cbf = [carve(R_C + 26720 + i * 1040, 1040, BF16, (2, 260)) for i in range(2)]
    tho = [carve(R_C + 28800 + i * 1024, 1024, F32) for i in range(2)]
    szm = [carve(R_C + 30848 + i * 1024, 1024, F32) for i in range(2)] + [carve(R_C + 38528, 1024, F32)]
    Ab = [carve(R_C + 32896 + i * 1024, 1024, F32) for i in range(2)]
    wTb = [carve(R_C + 34944 + i * 256, 256, BF16) for i in range(2)]
    ekb = [carve(R_C + 35456 + i * 512, 512, BF16) for i in range(2)]
    hmb = [carve(R_C + 36480 + i * 512, 512, BF16) for i in range(2)]
    tiny = [carve(R_C + 37504 + i * 64, 64, F32) for i in range(8)]
    junkm = carve(R_C + 38016, 512, BF16)
    assert R_C + 39552 <= ARENA

    PB = [nc.alloc_psum_tensor("pb%d" % i, [128, 512], F32) for i in range(8)]
    PBf = [p.ap() for p in PB]
    PBb = [p.bitcast(BF16).ap() for p in PB]

    out_toks = []
    with nc.Block() as block:
        K = Sched(nc, block)
        B = K.B

        def c_ones(e):
            e.memset(ones_f, 1.0)
            e.memset(onesb, 1.0 / 1024.0)
            e.memset(nhalf, -0.5)
            return e.memset(gt["carry"][:, 0:4], 0.0)
        K.op("pool", c_ones, writes=[B("ones")])
        K.op("pool", lambda e: e.affine_select(out=ident_f, in_=ones_f, pattern=[[-1, 128]],
                                               compare_op=ALU.is_equal, fill=0.0, base=0,
                                               channel_multiplier=1),
             reads=[B("ones")], writes=[B("ident_f")])
        K.op("pool", lambda e: e.affine_select(out=maskT, in_=ones_f, pattern=[[1, 128]],
                                               compare_op=ALU.is_ge, fill=0.0, base=0,
                                               channel_multiplier=-1),
             reads=[B("ones")], writes=[B("maskT")])
        K.op("pool", lambda e: e.tensor_copy(out=ident_bf, in_=ident_f),
             reads=[B("ident_f")], writes=[B("ident_bf")])

        def bc_row(ap_dram, n):
            return bass.AP(ap_dram.tensor, 0, [[0, 128], [1, n]])

        def col_vec(ap_dram):
            return bass.AP(ap_dram.tensor, 0, [[1, 128], [128, 8]])
        K.dma("act", lambda e: e.dma_start(out=gbc, in_=bc_row(norm_g, D)), writes=[B("gbc")])
        K.dma("act", lambda e: e.dma_start(out=cw[0:31, :], in_=conv_w), writes=[B("cw")])

        w_in_v = w_in.rearrange("(dc p) n -> p dc n", p=128)

        def load_w(slot, parts, name):
            tok = None
            for (dc0, sc0, ncol) in parts:
                tok = K.dma("pool", lambda e, dc0=dc0, sc0=sc0, ncol=ncol: e.dma_start(
                    out=wslot[slot][:, :, dc0:dc0 + ncol], in_=w_in_v[:, :, sc0:sc0 + ncol]),
                    writes=[B("w", slot)])
            return tok

        def p0_front(tt):
            s5, s3 = tt % 5, tt % 3
            K.dma("sp", lambda e: e.dma_start(out=xs0[s5], in_=x[tt * 128:(tt + 1) * 128, :]),
                  writes=[B("xs0", s5)])
            K.op("act", lambda e: e.activation(out=junk, in_=xs0[s5], func=AF.Square,
                                               accum_out=ssq[:, tt:tt + 1]),
                 reads=[B("xs0", s5)], writes=[B("junk"), B("ssq", tt)])
            K.op("act", lambda e: e.activation(out=srt[:, tt:tt + 1], in_=ssq[:, tt:tt + 1],
                                               func=AF.Sqrt, scale=1.0 / D, bias=EPS),
                 reads=[B("ssq", tt)], writes=[B("srt", tt)])
            K.op("dve", lambda e: e.reciprocal(out=rstd0[:, tt:tt + 1], in_=srt[:, tt:tt + 1]),
                 reads=[B("srt", tt)], writes=[B("rstd0", tt)])
            K.op("dve", lambda e: e.scalar_tensor_tensor(
                out=xn[s3], in0=xs0[s5], scalar=rstd0[:, tt:tt + 1], in1=gbc, op0=ALU.mult, op1=ALU.mult),
                reads=[B("xs0", s5), B("rstd0", tt), B("gbc")], writes=[B("xn", s3)])
            pt = PBb[6 + tt % 2]

            def tr(e):
                ins = None
                for dc in range(8):
                    ins = e.transpose(out=pt[:, dc * 128:(dc + 1) * 128],
                                      in_=xn[s3][:, dc * 128:(dc + 1) * 128], identity=ident_bf)
                return ins
            K.op("pe", tr, reads=[B("xn", s3), B("ident_bf")], writes=[B("ps", 6 + tt % 2)])

        def p0_back(tt):
            pt = PBb[6 + tt % 2]
            if False:
                f = lambda e: e.activation(
                    out=uT[:, :, tt * 128:(tt + 1) * 128], in_=pt.rearrange("p (a b) -> p a b", a=8), func=AF.Copy)
                K.op("act", f, reads=[B("ps", 6 + tt % 2)], writes=[B("uT", tt)])
            else:
                f = lambda e: e.tensor_copy(
                    out=uT[:, :, tt * 128:(tt + 1) * 128], in_=pt.rearrange("p (a b) -> p a b", a=8))
                K.op("dve", f, reads=[B("ps", 6 + tt % 2)], writes=[B("uT", tt)])
        load_w(0, [(0, C_GA, 128), (128, C_GG, 128)], "glu")

        def tile_proj(tt):
            sb = (tt // 4) % 2
            c0 = (tt % 4) * 128
            ws0 = wslot[0]

            def f(e):
                mm_group(e, PBf[2 + sb][:, c0:c0 + 128],
                         [(ws0[:, dc, 128:256], uT[:, dc, tt * 128:(tt + 1) * 128]) for dc in range(8)])
                return mm_group(e, PBf[sb][:, c0:c0 + 128],
                                [(ws0[:, dc, 0:128], uT[:, dc, tt * 128:(tt + 1) * 128]) for dc in range(8)])
            K.op("pe", f, reads=[B("uT", tt), B("w", 0)], writes=[B("ps", sb), B("ps", 2 + sb)])

        def tile_ev(tb):
            sb = tb % 2
            K.op("act", lambda e: e.activation(out=thb[sb], in_=PBf[2 + sb], func=AF.Tanh, scale=0.5),
                 reads=[B("ps", 2 + sb)], writes=[B("th", sb)])
            K.op("dve", lambda e: e.scalar_tensor_tensor(
                out=cbuf[0][:, 30 + tb * 512:30 + (tb + 1) * 512], in0=thb[sb], scalar=1.0, in1=PBf[sb],
                op0=ALU.add, op1=ALU.mult),
                reads=[B("th", sb), B("ps", sb)], writes=[B("cbuf", 0, tb)])
        pend_ev = []
        LAG = 6
        for i in range(NT + LAG + 3):
            if i < NT:
                p0_front(i)
            if 1 <= i <= NT:
                p0_back(i - 1)
            if 0 <= i - LAG < NT:
                tile_proj(i - LAG)
                if (i - LAG) % 4 == 3:
                    pend_ev.append((i + 2, (i - LAG) // 4))
            while pend_ev and pend_ev[0][0] <= i:
                tile_ev(pend_ev.pop(0)[1])
        while pend_ev:
            tile_ev(pend_ev.pop(0)[1])

        for nm, dst, src in (("convbT", convbT, conv_b), ("lngT", lngT, ln_g),
                             ("lnbT", lnbT, ln_b), ("mhgT", mhgT, mh_g)):
            K.dma("sp", lambda e, dst=dst, src=src: e.dma_start(
                out=dst, in_=col_vec(src), allow_slow_non_contiguous=True), writes=[B(nm)])
        K.dma("sp", lambda e: e.dma_start(
            out=bgbc.rearrange("p (t j) -> p t j", j=8),
            in_=bass.AP(b_gates.tensor, 0, [[0, 128], [0, 16], [1, 8]])), writes=[B("bgbc")])


        def trw(e):
            ins = None
            for cc in range(8):
                ins = e.transpose(out=PBf[5][:, cc * 32:cc * 32 + 31], in_=cw[0:31, cc * 128:(cc + 1) * 128],
                                  identity=ident_f[0:31, 0:31])
            return ins
        K.op("pe", trw, reads=[B("cw"), B("ident_f")], writes=[B("ps", 5)])
        K.op("pool", lambda e: e.memset(wconvT, 0.0), writes=[B("wconvT")])
        K.op("dve", lambda e: e.tensor_scalar(
            out=wconvT[:, :, 0:31], in0=PBf[5][:, 0:256].rearrange("p (a b) -> p a b", a=8)[:, :, 0:31],
            scalar1=0.5, scalar2=None, op0=ALU.mult), reads=[B("ps", 5)], writes=[B("wconvT")])

        def mk_sel0(e):
            ins = None
            for q in range(4):
                ins = e.memset(Pq[q], 0.0)
            return ins
        K.op("pool", mk_sel0, writes=[B("Pq")])

        def mk_sel(e):
            ins = None
            for q in range(4):
                ins = e.tensor_copy(out=Pq[q][:, 96:128], in_=ident_f[:, 32 * q:32 * q + 32])
            return ins
        K.op("pool", mk_sel, reads=[B("ident_f")], writes=[B("Pq")])
        K.op("pool", lambda e: e.tensor_copy(out=Rrep, in_=ident_bf[:, 0:32]), reads=[B("ident_bf")], writes=[B("Rrep")])
        for j in range(1, 4):
            K.op("pool", lambda e, j=j: e.tensor_tensor(out=Rrep, in0=Rrep, in1=ident_bf[:, 32 * j:32 * j + 32], op=ALU.add),
                 reads=[B("Rrep"), B("ident_bf")], writes=[B("Rrep")])

        def wcol_mm(e):
            ins = None
            for q in range(4):
                for jlo in range(4):
                    ins = e.matmul(PBf[5][:, q * 64:(q + 1) * 64], Pq[q][:, 96 - 32 * jlo:224 - 32 * jlo],
                                   wconvT[:, :, jlo:32:4], start=(jlo == 0), stop=(jlo == 3))
            return ins
        K.op("pe", wcol_mm, reads=[B("wconvT"), B("Pq")], writes=[B("ps", 5)])
        K.op("dve", lambda e: e.tensor_copy(
            out=wcolT.rearrange("p (c q g) -> p q c g", c=8, q=4),
            in_=PBf[5][:, 0:256].rearrange("p (q c g) -> p q c g", q=4, c=8)),
            reads=[B("ps", 5)], writes=[B("wcolT")])

        uT_all = [B("uT", t) for t in range(NT)]

        def uT_blk(tb):
            return [B("uT", t) for t in range(tb * 4, tb * 4 + 4)]

        Wg = carve(R_W + 5 * 4096, 4096, BF16, (8, 256))
        steps = []

        def st(fn):
            steps.append(fn)

        def st_pe(fn):
            steps.extend([None, None, None])
            steps.append(fn)
        G3 = Gt.rearrange("p (t j) -> p t j", j=8)
        ipre = G3[:, :, 0:4]
        fpre = G3[:, :, 4:8]

        def v3(ap):
            return ap.rearrange("p (t j) -> p t j", j=4)
        K.dma("pool", lambda e: e.dma_start(out=Wg[:, :, 0:8], in_=w_in_v[:, :, C_I:C_I + 8]),
              writes=[B("w", 5)])

        def gate_mm(e):
            ins = None
            for tt in range(NT):
                ins = mm_group(e, PBf[7][:, tt * 8:(tt + 1) * 8],
                               [(uT[:, dc, tt * 128:(tt + 1) * 128], Wg[:, dc, 0:8]) for dc in range(8)])
            return ins
        st_pe(lambda: K.op("pe", gate_mm, reads=uT_all + [B("w", 5)], writes=[B("ps", 7)]))
        st(lambda: K.op("dve", lambda e: e.tensor_tensor(out=Gt, in0=PBf[7][:, 0:128], in1=bgbc, op=ALU.add),
                        reads=[B("ps", 7), B("bgbc")], writes=[B("G")]))
        st(lambda: K.op("act", lambda e: e.activation(out=v3(gt["af"]), in_=fpre, func=AF.Abs),
                        reads=[B("G")], writes=[B("af")]))
        st(lambda: K.op("act", lambda e: e.activation(out=gt["ex"], in_=gt["af"], func=AF.Exp, scale=-1.0),
                        reads=[B("af")], writes=[B("ex")]))
        st(lambda: K.op("act", lambda e: e.activation(out=gt["l1"], in_=gt["ex"], func=AF.Ln, bias=1.0),
                        reads=[B("ex")], writes=[B("l1")]))
        st(lambda: K.op("dve", lambda e: e.scalar_tensor_tensor(
            out=v3(gt["lf"]), in0=fpre, scalar=0.0, in1=v3(gt["l1"]), op0=ALU.min, op1=ALU.subtract),
            reads=[B("G"), B("l1")], writes=[B("lf")]))

        def cs_mm(e):
            e.matmul(PBf[7][:, 128:192], ones_f, gt["lf"], start=True, stop=True)
            return e.matmul(PBf[7][:, 192:256], maskT, gt["lf"], start=True, stop=True)
        st_pe(lambda: K.op("pe", cs_mm, reads=[B("lf"), B("ones"), B("maskT")], writes=[B("ps", 7)]))
        for tt in range(1, NT):
            st(lambda tt=tt: K.op("dve", lambda e: e.tensor_tensor(
                out=gt["carry"][:, tt * 4:tt * 4 + 4], in0=gt["carry"][:, tt * 4 - 4:tt * 4],
                in1=PBf[7][:, 128 + tt * 4 - 4:128 + tt * 4], op=ALU.add),
                reads=[B("ps", 7), B("carry", tt - 1), B("ones")], writes=[B("carry", tt)]))
        st(lambda: K.op("dve", lambda e: e.tensor_tensor(out=gt["Bt"], in0=PBf[7][:, 192:256], in1=gt["carry"], op=ALU.add),
                        reads=[B("ps", 7)] + [B("carry", t) for t in range(1, NT)], writes=[B("Bt")]))
        st(lambda: K.op("dve", lambda e: e.tensor_tensor(out=v3(gt["a"]), in0=ipre, in1=v3(gt["Bt"]), op=ALU.subtract),
                        reads=[B("G"), B("Bt")], writes=[B("a")]))
        st_pe(lambda: K.op("pe", lambda e: e.transpose(out=PBf[7][0:64, 256:384], in_=gt["a"], identity=ident_f),
                        reads=[B("a"), B("ident_f")], writes=[B("ps", 7)]))
        st(lambda: K.op("dve", lambda e: e.reduce_max(out=cmax[0:64, 0:1], in_=PBf[7][0:64, 256:384], axis=AX.X),
                        reads=[B("ps", 7)], writes=[B("cmax")]))
        st_pe(lambda: K.op("pe", lambda e: e.transpose(out=PBf[7][0:1, 384:448], in_=cmax[0:64, 0:1],
                                                    identity=ident_f[0:64, 0:64]),
                        reads=[B("cmax"), B("ident_f")], writes=[B("ps", 7)]))
        st(lambda: K.op("dve", lambda e: e.tensor_copy(out=minc[0:1, 0:4], in_=PBf[7][0:1, 384:388]),
                        reads=[B("ps", 7)], writes=[B("minc", 0)]))
        for tt in range(1, NT):
            st(lambda tt=tt: K.op("dve", lambda e: e.tensor_tensor(
                out=minc[0:1, tt * 4:tt * 4 + 4], in0=minc[0:1, tt * 4 - 4:tt * 4],
                in1=PBf[7][0:1, 384 + tt * 4:388 + tt * 4], op=ALU.max),
                reads=[B("ps", 7), B("minc", tt - 1)], writes=[B("minc", tt)]))
        st_pe(lambda: K.op("pe", lambda e: e.matmul(PBf[7][:, 448:512], ones_f[0:1, 0:128], minc[0:1, 0:64],
                                                 start=True, stop=True),
                        reads=[B("minc", t) for t in range(NT)] + [B("ones")], writes=[B("ps", 7)]))
        st(lambda: K.op("dve", lambda e: e.tensor_copy(out=gt["Mbc"], in_=PBf[7][:, 448:512]),
                        reads=[B("ps", 7)], writes=[B("Mbc")]))
        st(lambda: K.op("dve", lambda e: e.tensor_tensor(out=gt["d"], in0=gt["a"], in1=gt["Mbc"], op=ALU.subtract),
                        reads=[B("a"), B("Mbc")], writes=[B("d")]))
        st(lambda: K.op("act", lambda e: e.activation(out=gt["e"], in_=gt["d"], func=AF.Exp),
                        reads=[B("d")], writes=[B("e")]))
        st(lambda: K.op("dve", lambda e: e.tensor_tensor(out=gt["d"][:, 0:60], in0=gt["a"][:, 0:60],
                                                         in1=gt["Mbc"][:, 4:64], op=ALU.subtract),
                        reads=[B("a"), B("Mbc")], writes=[B("d")]))
        st(lambda: K.op("act", lambda e: e.activation(out=gt["e2"][:, 0:60], in_=gt["d"][:, 0:60], func=AF.Exp),
                        reads=[B("d")], writes=[B("e2")]))
        st(lambda: K.op("dve", lambda e: e.tensor_tensor(out=gt["d"][:, 4:64], in0=gt["Mbc"][:, 0:60],
                                                         in1=gt["Mbc"][:, 4:64], op=ALU.subtract),
                        reads=[B("Mbc")], writes=[B("d")]))
        st(lambda: K.op("act", lambda e: e.activation(out=gt["r"][:, 4:64], in_=gt["d"][:, 4:64], func=AF.Exp),
                        reads=[B("d")], writes=[B("r")]))
        st(lambda: K.op("dve", lambda e: e.tensor_tensor(out=gt["d"], in0=gt["Bt"], in1=gt["Mbc"], op=ALU.add),
                        reads=[B("Bt"), B("Mbc")], writes=[B("d")]))
        st(lambda: K.op("act", lambda e: e.activation(out=gt["fl"], in_=gt["d"], func=AF.Exp, scale=-1.0),
                        reads=[B("d")], writes=[B("fl")]))

        def run_steps(n):
            for _ in range(n):
                if steps:
                    f_ = steps.pop(0)
                    if f_ is not None:
                        f_()

        items = [(cc, tb) for cc in range(8) for tb in range(NB)]

        def conv_load(cc):
            load_w(cc % 4, [(0, C_GA + cc * 128, 128), (128, C_GG + cc * 128, 128)], "glu")

        def build_wg(cc):
            s = cc % 2
            in0 = bass.AP(Rrep.tensor, Rrep.offset, [Rrep.ap[0], [0, 32], [1, 32]])
            wc = wcolT[:, cc * 32:(cc + 1) * 32]
            in1 = bass.AP(wc.tensor, wc.offset, [wc.ap[0], [1, 32], [0, 32]])
            K.op("pool", lambda e: e.tensor_tensor(out=Wgp[s], in0=in0, in1=in1, op=ALU.mult),
                 reads=[B("Rrep"), B("wcolT")], writes=[B("Wg", s)])

        conv_load(1)
        conv_load(2)
        y_fence = [(k, v) for k, v in K.cnt.items() if v > 0]

        def c_pad(e):
            for i in range(2):
                e.memset(cbuf[i][:, 0:30], 0.0)
                ins = e.memset(cbuf[i][:, 2078:2080], 0.0)
            return ins
        K.op("pool", c_pad, writes=[B("cpad")])
        build_wg(0)
        build_wg(1)

        def proj(n):
            cc, tb = items[n]
            if tb == 0:
                if cc + 2 < 8:
                    conv_load(cc + 2)
            s = n % 2
            ws = wslot[cc % 4]
            K.op("pe", lambda e: mm_group(e, PBf[s], [(ws[:, dc, 0:128], uT[:, dc, tb * 512:(tb + 1) * 512])
                                                      for dc in range(8)]),
                 reads=uT_blk(tb) + [B("w", cc % 4)], writes=[B("ps", s)])
            K.op("pe", lambda e: mm_group(e, PBf[2 + s], [(ws[:, dc, 128:256], uT[:, dc, tb * 512:(tb + 1) * 512])
                                                          for dc in range(8)]),
                 reads=uT_blk(tb) + [B("w", cc % 4)], writes=[B("ps", 2 + s)])
            K.op("act", lambda e: e.activation(out=thb[s], in_=PBf[2 + s], func=AF.Tanh, scale=0.5),
                 reads=[B("ps", 2 + s)], writes=[B("th", s)])
            cs = cc % 2
            K.op("dve", lambda e: e.scalar_tensor_tensor(
                out=cbuf[cs][:, 30 + tb * 512:30 + (tb + 1) * 512], in0=thb[s], scalar=1.0, in1=PBf[s],
                op0=ALU.add, op1=ALU.mult),
                reads=[B("th", s), B("ps", s)], writes=[B("cbuf", cs, tb)])

        def replicate(cc):
            cs = cc % 2
            rd = [B("cbuf", cs, tb) for tb in range(NB)] + [B("cpad")]
            for q in range(4):
                for j in range(4):
                    K.dma("pool" if q == 3 else "sp", lambda e, q=q, j=j: e.dma_start(
                        out=Xt[q][32 * j:32 * j + 32, 0:2076], in_=cbuf[cs][32 * q:32 * q + 32, j:j + 2076]),
                        reads=rd, writes=[B("Xt", q, j)])

        def conv(n):
            cc, tb = items[n]
            s = n % 2
            cs = cc % 2
            wg3 = Wgp[cs]

            def cmm(e):
                ins = None
                for g in range(8):
                    for q in range(4):
                        ins = e.matmul(PBf[4 + s][32 * q:32 * q + 32, :], wg3[:, q * 8 + g, :],
                                       Xt[q][:, tb * 512 + 4 * g:tb * 512 + 4 * g + 512],
                                       start=(g == 0), stop=(g == 7), tile_position=(0, 32 * q))
                return ins
            K.op("pe", cmm, reads=[B("Xt", q, j) for q in range(4) for j in range(4)] + [B("Wg", cs)],
                 writes=[B("ps", 4 + s)])
            K.op("act", lambda e: e.activation(out=ysl(cc, tb), in_=PBf[4 + s], func=AF.Identity,
                                               bias=convbT[:, cc:cc + 1]),
                 reads=[B("ps", 4 + s), B("convbT")], writes=[B("y", cc, tb)], extra=y_fence)

        def zc_load(cc):
            load_w(cc % 4, [(0, C_ZC + cc * 128, 128)], "zc")

        sctr = [0]

        def stat_item(tb, cc):
            ipm, ipq = tb * 2, tb * 2 + 1
            s = sctr[0] % 3
            s2 = sctr[0] % 2
            sctr[0] += 1
            K.op("dve", lambda e: e.tensor_copy(out=ybb[s], in_=ysl(cc, tb)),
                 reads=[B("y", cc, tb)], writes=[B("yb", s)])
            K.op("act", lambda e: e.activation(out=ysqb[s2], in_=ysl(cc, tb), func=AF.Square),
                 reads=[B("y", cc, tb)], writes=[B("ysq", s2)])

            def stat(e):
                e.matmul(PBf[ipm], onesb, ybb[s], start=(cc == 0), stop=(cc == 7))
                return e.matmul(PBf[ipq], onesb, ysqb[s2], start=(cc == 0), stop=(cc == 7))
            K.op("pe", stat, reads=[B("yb", s), B("ysq", s2), B("ones")],
                 writes=[B("ps", ipm), B("ps", ipq)])

        for cc in range(9):
            if 1 <= cc < 8:
                for tb in range(NB):
                    proj(cc * 4 + tb)
                    run_steps(3)
            if cc == 8:
                run_steps(1000)
                zc_load(0)
                zc_load(1)
                for tb in (0, 1, 3):
                    for c2 in range(7):
                        stat_item(tb, c2)
            if cc >= 1:
                for tb in range(NB):
                    conv((cc - 1) * 4 + tb)
                if cc + 1 < 8:
                    build_wg(cc + 1)
            if cc < 8:
                replicate(cc)
        for tb in (0, 1, 3):
            stat_item(tb, 7)
        for c2 in range(8):
            stat_item(2, c2)
        def _sl(tb):
            return slice(tb * 512, (tb + 1) * 512)
        for tb in range(NB):
            K.op("dve", lambda e, tb=tb: e.tensor_copy(out=mu[:, _sl(tb)], in_=PBf[tb * 2]),
                 reads=[B("ps", tb * 2)], writes=[B("mu", tb)])
            K.op("act", lambda e, tb=tb: e.activation(out=var[:, _sl(tb)], in_=PBf[tb * 2], func=AF.Square),
                 reads=[B("ps", tb * 2)], writes=[B("var", tb)])
        for tb in range(NB):
            K.op("dve", lambda e, tb=tb: e.tensor_tensor(out=var[:, _sl(tb)], in0=PBf[tb * 2 + 1], in1=var[:, _sl(tb)],
                                                        op=ALU.subtract),
                 reads=[B("ps", tb * 2 + 1), B("var", tb)], writes=[B("var", tb)])
        for tb in range(NB):
            K.op("act", lambda e, tb=tb: e.activation(out=var[:, _sl(tb)], in_=var[:, _sl(tb)], func=AF.Ln, bias=EPS),
                 reads=[B("var", tb)], writes=[B("var", tb)])
            K.op("act", lambda e, tb=tb: e.activation(out=var[:, _sl(tb)], in_=var[:, _sl(tb)], func=AF.Exp, scale=-0.5),
                 reads=[B("var", tb)], writes=[B("var", tb)])

        def head_loads(h, part=None):
            ozs = (4, 5) if h % 2 == 0 else (6, 3)
            jobs = [(0, C_Q, "q"), (1, C_K, "k"), (2, C_V, "v"), (ozs[0], C_O, "o"), (ozs[1], C_ZM, "z")]
            for k, (slot, col, nm) in enumerate(jobs):
                if part is None or part == k:
                    load_w(slot, [(0, col + h * 256, 256)], nm)
        nitems = [(cc, tb) for cc in range(8) for tb in range(NB)]

        def n_t1(n):
            cc, tb = nitems[n]
            if tb == 0 and cc + 2 < 8:
                zc_load(cc + 2)
            s = n % 4
            sl = slice(tb * 512, (tb + 1) * 512)
            K.op("dve", lambda e: e.tensor_tensor(out=tnb[s], in0=ysl(cc, tb), in1=mu[:, sl], op=ALU.subtract),
                 reads=[B("y", cc, tb), B("mu", tb)], writes=[B("tn_", s)])
            eng = "pool" if n % 2 == 0 else "dve"
            K.op(eng, lambda e: e.tensor_tensor(out=tnb[s], in0=tnb[s], in1=var[:, sl], op=ALU.mult),
                 reads=[B("tn_", s), B("var", tb)], writes=[B("tn_", s)])

        def n_mid(n):
            cc, tb = nitems[n]
            s = n % 4
            p = n % 2
            ws = wslot[cc % 4]
            K.op("pe", lambda e: mm_group(
                e, PBf[4 + p], [(ws[:, dc, 0:128], uT[:, dc, tb * 512:(tb + 1) * 512]) for dc in range(8)]),
                reads=uT_blk(tb) + [B("w", cc % 4)], writes=[B("ps", 4 + p)])
            K.op("act", lambda e: e.activation(out=szb[n % 3], in_=PBf[4 + p], func=AF.Silu),
                 reads=[B("ps", 4 + p)], writes=[B("sz", n % 3)])
            K.op("act", lambda e: e.activation(out=tnb[s], in_=tnb[s], func=AF.Silu,
                                               scale=lngT[:, cc:cc + 1], bias=lnbT[:, cc:cc + 1]),
                 reads=[B("tn_", s), B("lngT"), B("lnbT")], writes=[B("tn_", s)])

        def n_fin(n):
            cc, tb = nitems[n]
            s = n % 4
            sl = slice(tb * 512, (tb + 1) * 512)
            K.op("dve", lambda e: e.tensor_tensor(out=mixT[:, 8 + cc, sl], in0=tnb[s], in1=szb[n % 3], op=ALU.mult),
                 reads=[B("tn_", s), B("sz", n % 3)], writes=[B("mix", 8 + cc, tb)])
        for i in range(len(nitems) + 2):
            hl = {10: 3, 14: 4, 22: 0, 26: 1, 30: 2}
            if i in hl:
                head_loads(0, part=hl[i])
            if i < len(nitems):
                n_t1(i)
            if 0 <= i - 1 < len(nitems):
                n_mid(i - 1)
            if 0 <= i - 2 < len(nitems):
                n_fin(i - 2)
        K.barrier()

        for ec in range(16):
            K.dma("pool", lambda e, ec=ec: e.dma_start(out=woutT[:, ec, :], in_=w_out[ec * 128:(ec + 1) * 128, :]),
                  writes=[B("wout", ec)])
        K.op("pool", lambda e: e.memset(vext[:, :, 256:257], 1.0), writes=[B("vone")])
        for ec in range(8):
            K.op("pool", lambda e, ec=ec: e.tensor_scalar(
                out=woutT[:, ec, :], in0=woutT[:, ec, :], scalar1=mhgT[:, ec:ec + 1], scalar2=1.0,
                op0=ALU.mult, op1=ALU.mult),
                reads=[B("wout", ec), B("mhgT")], writes=[B("wout", ec)])
        e_, e2_, r_, fl_ = gt["e"], gt["e2"], gt["r"], gt["fl"]
        gate_bufs = [B("e"), B("e2"), B("r"), B("fl")]

        pnc = [0]

        tails = []

        def run_tail():
            if tails:
                tails.pop(0)()

        def head_body(h):
            pn = pnc[0]
            oz = (4, 5) if h % 2 == 0 else (6, 3)
            wo, wz = wslot[oz[0]], wslot[oz[1]]
            for which in range(2):
                dst = qT if which == 0 else kT
                nm = "qT" if which == 0 else "kT"
                ws = wslot[which]
                scl = 1.0 if which == 0 else 1.0 / 16.0
                for half in range(2):
                    for tb in range(NB):
                        pb = pn % 4
                        pn += 1
                        sl = slice(tb * 512, (tb + 1) * 512)
                        K.op("pe", lambda e, ws=ws, half=half, tb=tb, pb=pb: mm_group(
                            e, PBf[pb], [(ws[:, dc, half * 128:(half + 1) * 128], uT[:, dc, tb * 512:(tb + 1) * 512])
                                         for dc in range(8)]),
                            reads=uT_blk(tb) + [B("w", which)], writes=[B("ps", pb)])
                        if pn % 2 == 0:
                            K.op("act", lambda e, dst=dst, half=half, sl=sl, pb=pb, scl=scl: e.activation(
                                out=dst[:, half, sl], in_=PBf[pb], func=AF.Copy, scale=scl),
                                reads=[B("ps", pb)], writes=[B(nm, half, tb)])
                        else:
                            K.op("dve", lambda e, dst=dst, half=half, sl=sl, pb=pb, scl=scl: e.tensor_scalar(
                                out=dst[:, half, sl], in0=PBf[pb], scalar1=scl, scalar2=None, op0=ALU.mult),
                                reads=[B("ps", pb)], writes=[B(nm, half, tb)])
                        if which * 8 + half * 4 + tb + 1 in (5, 9):
                            run_tail()
            for tt in range(NT):
                pb = pn % 4
                pn += 1
                K.op("pe", lambda e, tt=tt, pb=pb: mm_group(
                    e, PBf[pb][:, 0:256], [(uT[:, dc, tt * 128:(tt + 1) * 128], wslot[2][:, dc, 0:256])
                                           for dc in range(8)]),
                    reads=[B("uT", tt), B("w", 2)], writes=[B("ps", pb)])
                if pn % 2 == 0:
                    K.op("act", lambda e, tt=tt, pb=pb: e.activation(out=vext[:, tt, 0:256], in_=PBf[pb][:, 0:256],
                                                                     func=AF.Copy),
                         reads=[B("ps", pb)], writes=[B("v", tt)])
                else:
                    K.op("dve", lambda e, tt=tt, pb=pb: e.tensor_copy(out=vext[:, tt, 0:256], in_=PBf[pb][:, 0:256]),
                         reads=[B("ps", pb)], writes=[B("v", tt)])
            def gcol_(c):
                return slice(c * 4 + h, c * 4 + h + 1)

            def stage_A(c):
                s = c % 2
                ct = slice(c * 128, (c + 1) * 128)
                tbc = c // 4
                gcol = gcol_(c)

                def ozmm(e):
                    mm_group(e, PBf[s][:, 0:256], [(uT[:, dc, ct], wo[:, dc, 0:256]) for dc in range(8)])
                    return mm_group(e, PBf[s][:, 256:512], [(uT[:, dc, ct], wz[:, dc, 0:256]) for dc in range(8)])
                K.op("pe", ozmm, reads=[B("uT", c), B("w", oz[0]), B("w", oz[1])], writes=[B("ps", s)])
                K.op("act", lambda e: e.activation(out=tho[s], in_=PBf[s][:, 0:256], func=AF.Tanh, scale=0.5),
                     reads=[B("ps", s)], writes=[B("tho", s)])
                s3 = c % 3
                K.op("act", lambda e: e.activation(out=szm[s3], in_=PBf[s][:, 256:512], func=AF.Silu),
                     reads=[B("ps", s)], writes=[B("szm", s3)])
                K.op("pe", lambda e: mm_group(
                    e, PBf[2 + s][:, 384:512], [(kT[:, half, ct], qT[:, half, ct]) for half in range(2)]),
                    reads=[B("kT", 0, tbc), B("kT", 1, tbc), B("qT", 0, tbc), B("qT", 1, tbc)],
                    writes=[B("ps", 2 + s)])
                K.op("dve", lambda e: e.scalar_tensor_tensor(
                    out=wTb[s], in0=PBf[2 + s][:, 384:512], scalar=e_[:, gcol], in1=maskT,
                    op0=ALU.mult, op1=ALU.mult),
                    reads=[B("ps", 2 + s), B("e"), B("maskT")], writes=[B("wT", s)])
                if c < NT - 1:
                    def ktr(e):
                        ins = None
                        for half in range(2):
                            ins = e.transpose(out=PBb[6][:, half * 128:(half + 1) * 128],
                                              in_=kT[:, half, ct], identity=ident_bf)
                        return ins
                    K.op("pe", ktr, reads=[B("kT", 0, tbc), B("kT", 1, tbc), B("ident_bf")], writes=[B("ps", 6)])
                    K.op("act", lambda e: e.activation(
                        out=ekb[s], in_=PBb[6][:, 0:256], func=AF.Copy, scale=e2_[:, gcol]),
                        reads=[B("ps", 6), B("e2")], writes=[B("ek", s)])

            def stage_O3(c):
                s = c % 2
                cc1 = slice(c, c + 1)
                K.op("dve", lambda e: e.scalar_tensor_tensor(
                    out=hmb[s], in0=Ab[s], scalar=tiny[6][:, cc1], in1=szm[c % 3], op0=ALU.mult, op1=ALU.mult),
                    reads=[B("A", s), B("tn", 6, c), B("szm", c % 3)], writes=[B("hm", s)])

            def stage_B(c):
                s = c % 2
                ct = slice(c * 128, (c + 1) * 128)
                tbc = c // 4
                gcol = gcol_(c)
                cc1 = slice(c, c + 1)
                cb = c % 2
                pairs = [(wTb[s], vext[:, c, 0:257])]
                rds = [B("wT", s), B("v", c), B("vone")]
                if c > 0:
                    pairs += [(qT[:, half, ct], Cbf[cb][:, half, 0:257]) for half in range(2)]
                    rds += [B("qT", 0, tbc), B("qT", 1, tbc), B("Cbf", cb, 0), B("Cbf", cb, 1)]
                if c < NT - 1:
                    def kvmm(e):
                        e.matmul(PBf[4][:, 0:257], ekb[s][:, 0:128], vext[:, c, 0:257], start=True, stop=True)
                        return e.matmul(PBf[5][:, 0:257], ekb[s][:, 128:256], vext[:, c, 0:257], start=True, stop=True)
                    K.op("pe", kvmm, reads=[B("ek", s), B("v", c), B("vone")], writes=[B("ps", 4), B("ps", 5)])
                K.op("pe", lambda e: mm_group(e, PBf[2 + s][:, 0:257], pairs), reads=rds, writes=[B("ps", 2 + s)])
                if c < NT - 1:
                    rcol = slice((c + 1) * 4 + h, (c + 1) * 4 + h + 1)
                    nb = (c + 1) % 2
                    for half in range(2):
                        if c == 0:
                            K.op("dve", lambda e, half=half: e.tensor_copy(out=Cm[:, half, 0:257],
                                                                           in_=PBf[4 + half][:, 0:257]),
                                 reads=[B("ps", 4 + half)], writes=[B("Cm", half)])
                        else:
                            K.op("dve", lambda e, half=half: e.scalar_tensor_tensor(
                                out=Cm[:, half, 0:257], in0=Cm[:, half, 0:257], scalar=r_[:, rcol],
                                in1=PBf[4 + half][:, 0:257], op0=ALU.mult, op1=ALU.add),
                                reads=[B("ps", 4 + half), B("Cm", half), B("r")], writes=[B("Cm", half)])
                        K.op("act", lambda e, half=half: e.activation(out=Cbf[nb][:, half, 0:257],
                                                                      in_=Cm[:, half, 0:257], func=AF.Copy),
                             reads=[B("Cm", half)], writes=[B("Cbf", nb, half)])
                K.op("dve", lambda e: e.tensor_scalar(
                    out=tiny[7][:, cc1], in0=PBf[2 + s][:, 256:257], scalar1=fl_[:, gcol], scalar2=None,
                    op0=ALU.max),
                    reads=[B("ps", 2 + s), B("fl")], writes=[B("tn", 7, c)])
                K.op("dve", lambda e: e.scalar_tensor_tensor(
                    out=tiny[0][:, cc1], in0=PBf[2 + s][:, 256:257], scalar=-1.0, in1=tiny[7][:, cc1],
                    op0=ALU.mult, op1=ALU.max),
                    reads=[B("ps", 2 + s), B("tn", 7, c)], writes=[B("tn", 0, c)])
                K.op("dve", lambda e: e.reciprocal(out=tiny[1][:, cc1], in_=tiny[0][:, cc1]),
                     reads=[B("tn", 0, c)], writes=[B("tn", 1, c)])
                K.op("dve", lambda e: e.scalar_tensor_tensor(
                    out=Ab[s], in0=tho[s], scalar=1.0, in1=PBf[2 + s][:, 0:256], op0=ALU.add, op1=ALU.mult),
                    reads=[B("tho", s), B("ps", 2 + s)], writes=[B("A", s)])
                K.op("act", lambda e: e.activation(out=junkm, in_=Ab[s], func=AF.Square,
                                                   accum_out=tiny[2][:, cc1]),
                     reads=[B("A", s)], writes=[B("junkm"), B("tn", 2, c)])

                K.op("pool", lambda e: e.tensor_scalar(
                    out=tiny[3][:, cc1], in0=tiny[2][:, cc1], scalar1=tiny[1][:, cc1],
                    scalar2=tiny[1][:, cc1], op0=ALU.mult, op1=ALU.mult),
                    reads=[B("tn", 2, c), B("tn", 1, c)], writes=[B("tn", 3, c)])
                K.op("pool", lambda e: e.tensor_scalar(
                    out=tiny[4][:, cc1], in0=tiny[3][:, cc1], scalar1=0.25 / 256.0, scalar2=EPS,
                    op0=ALU.mult, op1=ALU.add),
                    reads=[B("tn", 3, c)], writes=[B("tn", 4, c)])
                K.op("pool", lambda e: e.tensor_tensor(out=tiny[5][:, cc1], in0=tiny[4][:, cc1], in1=nhalf[:, 0:1],
                                                       op=ALU.pow),
                     reads=[B("tn", 4, c), B("ones")], writes=[B("tn", 5, c)])
                K.op("pool", lambda e: e.tensor_scalar(
                    out=tiny[6][:, cc1], in0=tiny[5][:, cc1], scalar1=tiny[1][:, cc1],
                    scalar2=0.5, op0=ALU.mult, op1=ALU.mult),
                    reads=[B("tn", 5, c), B("tn", 1, c)], writes=[B("tn", 6, c)])

            def stage_C(c):
                s = c % 2
                ct = slice(c * 128, (c + 1) * 128)

                def htr(e):
                    ins = None
                    for half in range(2):
                        ins = e.transpose(out=PBb[7][:, half * 128:(half + 1) * 128],
                                          in_=hmb[s][:, half * 128:(half + 1) * 128], identity=ident_bf)
                    return ins
                K.op("pe", htr, reads=[B("hm", s), B("ident_bf")], writes=[B("ps", 7)])

                K.op("act", lambda e: e.activation(
                    out=mixT[:, h * 2:h * 2 + 2, ct], in_=PBb[7][:, 0:256].rearrange("p (a b) -> p a b", a=2),
                    func=AF.Copy),
                    reads=[B("ps", 7)], writes=[B("mix", h * 2, c), B("mix", h * 2 + 1, c)])

            def step(i):
                if 2 <= i < 7 and h + 1 < 4:
                    head_loads(h + 1, part=i - 2)
                if 0 <= i + 1 < NT:
                    stage_A(i + 1)
                if 0 <= i < NT:
                    stage_B(i)
                if 0 <= i - 1 < NT:
                    stage_O3(i - 1)
                if 0 <= i - 2 < NT:
                    stage_C(i - 2)
            for i in range(-1, NT):
                step(i)
            tails.append(lambda: step(NT))
            tails.append(lambda: step(NT + 1))
            pnc[0] = pn
        for h in range(4):
            head_body(h)
        while tails:
            run_tail()
        K.barrier()

        K.dma("sp", lambda e: e.dma_start(out=gbc, in_=bc_row(fin_g, D)), writes=[B("gbc")])
        def x_load4(tt):
            K.dma("sp", lambda e, tt=tt: e.dma_start(out=xs[tt % 3], in_=x[tt * 128:(tt + 1) * 128, :]),
                  writes=[B("xs", tt % 3)])
        x_load4(0)
        x_load4(1)
        for tt in range(NT):
            s3, s2 = tt % 3, tt % 2
            ct = slice(tt * 128, (tt + 1) * 128)
            if tt + 2 < NT:
                x_load4(tt + 2)
            for nh in range(2):
                pb = s2 * 2 + nh
                K.op("pe", lambda e, pb=pb, ct=ct, nh=nh: mm_group(
                    e, PBf[pb], [(mixT[:, ec, ct], woutT[:, ec, nh * 512:(nh + 1) * 512]) for ec in range(16)]),
                    reads=[B("wout", ec) for ec in range(16)], writes=[B("ps", pb)])
                K.op("dve", lambda e, pb=pb, s2=s2, s3=s3, nh=nh: e.tensor_tensor(
                    out=hres[s2][:, nh * 512:(nh + 1) * 512], in0=PBf[pb], in1=xs[s3][:, nh * 512:(nh + 1) * 512],
                    op=ALU.add),
                    reads=[B("ps", pb), B("xs", s3)], writes=[B("hres", s2, nh)])
            K.op("act", lambda e, s2=s2, tt=tt: e.activation(out=junk4, in_=hres[s2], func=AF.Square,
                                                             accum_out=ssq4[:, tt:tt + 1]),
                 reads=[B("hres", s2, 0), B("hres", s2, 1)], writes=[B("junk4"), B("ssq4", tt)])
            K.op("act", lambda e, tt=tt: e.activation(out=srt4[:, tt:tt + 1], in_=ssq4[:, tt:tt + 1],
                                                      func=AF.Sqrt, scale=1.0 / D, bias=EPS),
                 reads=[B("ssq4", tt)], writes=[B("srt4", tt)])
            K.op("dve", lambda e, tt=tt: e.reciprocal(out=rstd4[:, tt:tt + 1], in_=srt4[:, tt:tt + 1]),
                 reads=[B("srt4", tt)], writes=[B("rstd4", tt)])
            K.op("dve", lambda e, s2=s2, tt=tt: e.scalar_tensor_tensor(
                out=outt[s2], in0=hres[s2], scalar=rstd4[:, tt:tt + 1], in1=gbc, op0=ALU.mult, op1=ALU.mult),
                reads=[B("hres", s2, 0), B("hres", s2, 1), B("rstd4", tt), B("gbc")], writes=[B("outt", s2)])
            out_toks.append(K.dma("sp", lambda e, tt=tt, s2=s2: e.dma_start(out=out[tt * 128:(tt + 1) * 128, :],
                                                                            in_=outt[s2]),
                                  reads=[B("outt", s2)]))
        K.wait_all("sp", out_toks)
        if debug:
            K.barrier()
            dcon = nc.dram_tensor("d_const", [128, 12288], U8, kind="ExternalOutput").ap()
            dut = nc.dram_tensor("d_uT", [128, 8 * 2048], BF16, kind="ExternalOutput").ap()
            dmix = nc.dram_tensor("d_mixT", [128, 16 * 2048], BF16, kind="ExternalOutput").ap()
            t1 = K.dma("sp", lambda e: e.dma_start(out=dcon, in_=arena[:, 0:12288]))
            t2 = K.dma("sp", lambda e: e.dma_start(out=dut, in_=arena[:, R_UT:R_UT + 32768].bitcast(BF16)))
            t3 = K.dma("sp", lambda e: e.dma_start(out=dmix, in_=arena[:, R_MIX:R_MIX + 65536].bitcast(BF16)))
            K.wait_all("sp", [t1, t2, t3])
        K.flush()
    if debug:
        offs = {}
        names = ["ident_bf", "ident_f", "maskT", "ones_f", "onesb", "convbT", "lngT", "lnbT", "mhgT", "nhalf", "Rrep",
                 "wconvT", "gbc", "Gt", "bgbc", "af", "ex", "l1", "lf", "carry", "Bt", "a", "Mbc", "e", "e2", "r",
                 "fl", "d", "cmax", "minc"]
        for nm, (off, nb) in zip(names, carve_log):
            offs[nm] = (off, nb)
        nc._dbg_offs = offs
    return nc


_NC = None


def kernel(x, norm_g, w_in, b_gates, mh_norm_g, conv_w, conv_b, conv_ln_g, conv_ln_b, w_out, final_norm_g):
    global _NC
    if _NC is None:
        _NC = build()
    nc = _NC
    f = lambda a: np.ascontiguousarray(np.asarray(a, dtype=np.float32))
    shared = {
        "norm_g": f(norm_g).reshape(1, D), "w_in": f(w_in).reshape(D, DIN), "b_gates": f(b_gates).reshape(1, 8),
        "mh_norm_g": f(mh_norm_g).reshape(1, D), "conv_w": f(conv_w).reshape(31, D),
        "conv_b": f(conv_b).reshape(1, D), "conv_ln_g": f(conv_ln_g).reshape(1, D),
        "conv_ln_b": f(conv_ln_b).reshape(1, D), "w_out": f(w_out).reshape(2 * D, D),
        "final_norm_g": f(final_norm_g).reshape(1, D),
    }
    xx = f(x)
    in_maps = [dict(shared, x=xx[b]) for b in range(8)]
    res = run_bass_kernel_spmd(nc, in_maps, core_ids=list(range(8)))
    return np.stack([np.asarray(r["out"], dtype=np.float32).reshape(S, D) for r in res.results], axis=0)
```
